# Optimizing a Trainium2 kernel written in Bass

```python
import math
import jax
import jax.numpy as jnp
from jax import lax
import numpy as np

D_MODEL = 1024
BATCH = 1
SEQ = 16384
DEPTH = 2

GRID_W = 64
CTX_LEN = 256
EPS = 1e-6
POOL_GROUPS = 4
POOL_WINDOWS = (2, 4, 8, 16)
POOL_WIDTH = D_MODEL // 4
POOL_GROUP_DIM = POOL_WIDTH // POOL_GROUPS
HEAD_DIM = 128
N_Q_HEADS = (D_MODEL - POOL_WIDTH) // HEAD_DIM
N_KV_HEADS = 2
ATT_WIDTH = N_Q_HEADS * HEAD_DIM
KV_WIDTH = N_KV_HEADS * HEAD_DIM
Q_BLOCK = 128
ROPE_THETA = 10000.0
ROPE_FREQS = HEAD_DIM // 4
EVEN_SPLITS = (POOL_WIDTH, 2 * POOL_WIDTH, 2 * POOL_WIDTH + ATT_WIDTH,
               2 * POOL_WIDTH + ATT_WIDTH + KV_WIDTH, 2 * POOL_WIDTH + ATT_WIDTH + 2 * KV_WIDTH)
EVEN_IN = 2 * POOL_WIDTH + 2 * ATT_WIDTH + 2 * KV_WIDTH
EVEN_MIX = POOL_WIDTH + ATT_WIDTH
HY_ORDER = 2
HY_WIDTH = 3 * D_MODEL // 4
HY_EMB = 33
HY_BANDS = (HY_EMB - 1) // 2
HY_HIDDEN = 64
HY_DECAY_TARGET = 1e-2
HY_FAST_DECAY = 0.3
HY_SLOW_DECAY = 1.5
FN_WIDTH = D_MODEL - HY_WIDTH
ODD_SPLITS = ((HY_ORDER + 1) * HY_WIDTH, (HY_ORDER + 2) * HY_WIDTH,
              (HY_ORDER + 2) * HY_WIDTH + FN_WIDTH)
ODD_IN = (HY_ORDER + 2) * HY_WIDTH + 2 * FN_WIDTH
ODD_MIX = HY_WIDTH + FN_WIDTH
N_EVEN = (DEPTH + 1) // 2
N_ODD = DEPTH // 2

kernel_name = "hybrid_pool_attn_hyena_fourier_dit"


def rmsnorm(x, g):
    xf = x.astype(jnp.float32)
    y = xf * lax.rsqrt(jnp.mean(xf * xf, axis=-1, keepdims=True) + EPS)
    return y.astype(x.dtype) * g


def modulation(cond, w_mod, b_mod):
    m = jax.nn.silu(cond) @ w_mod + b_mod
    return jnp.split(m, 3, axis=-1)


def to_heads(t, n_heads):
    return t.reshape(t.shape[:2] + (n_heads, HEAD_DIM))


def axial_rope_tables(row_idx, col_idx):
    inv_freq = ROPE_THETA ** (-jnp.arange(ROPE_FREQS, dtype=jnp.float32) / ROPE_FREQS)
    ang = jnp.stack([row_idx[:, None] * inv_freq, col_idx[:, None] * inv_freq], axis=1)
    ang = ang[:, None, :, None, :]
    return jnp.cos(ang), jnp.sin(ang)


def apply_rope(x, cos, sin):
    xr = x.reshape(x.shape[:-1] + (2, 2, ROPE_FREQS)).astype(jnp.float32)
    rot = jnp.concatenate([-xr[..., 1:, :], xr[..., :1, :]], axis=-2)
    return (xr * cos + rot * sin).astype(x.dtype).reshape(x.shape)


def blocked_attention(q, k, v):
    b, lq, hq, dh = q.shape
    hkv = k.shape[2]
    grp = hq // hkv
    nb = lq // Q_BLOCK
    qb = q.reshape(b, nb, Q_BLOCK, hkv, grp, dh).transpose(1, 0, 2, 3, 4, 5)
    scale = dh ** -0.5

    def one_block(qblk):
        s = jnp.einsum('bqhgd,bkhd->bhgqk', qblk, k, preferred_element_type=jnp.float32) * scale
        p = jax.nn.softmax(s, axis=-1).astype(v.dtype)
        return jnp.einsum('bhgqk,bkhd->bqhgd', p, v)

    o = lax.map(one_block, qb)
    return o.transpose(1, 0, 2, 3, 4, 5).reshape(b, lq, hq * dh)


def pool_mixer(u, w_grp, scale):
    b, L, _ = u.shape
    ug = u.reshape(b, L, POOL_GROUPS, POOL_GROUP_DIM).astype(jnp.float32)
    cs = jnp.concatenate([jnp.zeros((b, 1, POOL_GROUPS, POOL_GROUP_DIM), jnp.float32),
                          jnp.cumsum(ug, axis=1)], axis=1)
    t = jnp.arange(L)
    outs = []
    for gi, w in enumerate(POOL_WINDOWS):
        lo = jnp.clip(t - w // 2, 0, L)
        hi = jnp.clip(t + w // 2, 0, L)
        csg = cs[:, :, gi]
        win_sum = jnp.take(csg, hi, axis=1) - jnp.take(csg, lo, axis=1)
        cnt = (hi - lo).astype(jnp.float32)[None, :, None]
        outs.append(win_sum / cnt - ug[:, :, gi])
    d = jnp.stack(outs, axis=2).astype(u.dtype)
    y = jnp.einsum('blgc,gcd->blgd', d, w_grp)
    return y.reshape(b, L, POOL_WIDTH) * scale


def even_mix(a_val, a_gate, q, b_gate, k_all, v_all, pool_w, pool_scale, w_out):
    ya = pool_mixer(a_val, pool_w, pool_scale)
    yb = blocked_attention(q, k_all, v_all)
    y = jnp.concatenate([ya * jax.nn.silu(a_gate), yb * jax.nn.silu(b_gate)], axis=-1)
    return y @ w_out


def short_conv3(u, w, b):
    up = jnp.pad(u, ((0, 0), (1, 1), (0, 0)))
    return up[:, :-2] * w[0] + up[:, 1:-1] * w[1] + up[:, 2:] * w[2] + b


def hyena_filters(L, w1, b1, w2, b2, w3, freq):
    f32 = jnp.float32
    t = jnp.linspace(0.0, 1.0, L, dtype=f32)[:, None]
    w = 2.0 * math.pi * jnp.arange(L, dtype=f32)[:, None] / L
    f = jnp.linspace(1e-4, HY_BANDS - 1, HY_BANDS, dtype=f32)[None, :]
    emb = jnp.concatenate([t, jnp.cos(f * w), -jnp.sin(f * w)], axis=-1)
    fr = freq.astype(f32)
    hdn = jnp.sin(fr * (emb @ w1.astype(f32) + b1.astype(f32)))
    hdn = jnp.sin(fr * (hdn @ w2.astype(f32) + b2.astype(f32)))
    h = (hdn @ w3.astype(f32)).reshape(L, HY_ORDER, 2, HY_WIDTH)
    max_decay = math.log(HY_DECAY_TARGET) / HY_FAST_DECAY
    min_decay = math.log(HY_DECAY_TARGET) / HY_SLOW_DECAY
    deltas = jnp.linspace(min_decay, max_decay, HY_WIDTH, dtype=f32)
    decay = jnp.exp(-t * jnp.abs(deltas))
    h = h * decay[:, None, None, :]
    h = h / (jnp.sum(jnp.abs(h), axis=(0, 2), keepdims=True) + EPS)
    zero = jnp.zeros((1, HY_ORDER, HY_WIDTH), f32)
    return jnp.concatenate([h[:, :, 0], zero, h[:0:-1, :, 1]], axis=0)


def fft_long_conv(z, kfull):
    L = z.shape[1]
    zf = jnp.fft.rfft(z, n=2 * L, axis=1)
    kf = jnp.fft.rfft(kfull, n=2 * L, axis=0)
    return jnp.fft.irfft(zf * kf[None], n=2 * L, axis=1)[:, :L]


def hyena_mixer(u, conv_w, conv_b, w1, b1, w2, b2, w3, freq, skip):
    L = u.shape[1]
    uc = short_conv3(u, conv_w, conv_b)
    v, x1, x2 = jnp.split(uc, HY_ORDER + 1, axis=-1)
    k = hyena_filters(L, w1, b1, w2, b2, w3, freq)
    z = v.astype(jnp.float32)
    for o, gate_o in enumerate((x1, x2)):
        z = gate_o.astype(jnp.float32) * (fft_long_conv(z, k[:, o]) + skip[o].astype(jnp.float32) * z)
    return z.astype(u.dtype)


def fourier_mixer(u, w):
    y = jnp.fft.fft2(u.astype(jnp.float32), axes=(1, 2), norm='ortho').real
    return y.astype(u.dtype) @ w


def odd_mix(h, w_in, w_out, fn_w, conv_w, conv_b, w1, b1, w2, b2, w3, freq, skip):
    hy_in, hy_gate, fn_in, fn_gate = jnp.split(h @ w_in, ODD_SPLITS, axis=-1)
    yc = hyena_mixer(hy_in, conv_w, conv_b, w1, b1, w2, b2, w3, freq, skip)
    yd = fourier_mixer(fn_in, fn_w)
    y = jnp.concatenate([yc * jax.nn.silu(hy_gate), yd * jax.nn.silu(fn_gate)], axis=-1)
    return y @ w_out


def setup_inputs(seed: int = 0) -> dict:
    key = jax.random.key(seed)
    ks = jax.random.split(key, 26)

    def nrm(k, shape, s):
        return jax.random.normal(k, shape, jnp.float32) * s

    return {
        'x': nrm(ks[0], (BATCH, SEQ, D_MODEL), 1.0),
        'c': nrm(ks[1], (BATCH, D_MODEL), 1.0),
        'ctx': nrm(ks[2], (BATCH, CTX_LEN, D_MODEL), 1.0),
        'c_ctx': nrm(ks[3], (D_MODEL,), 1.0),
        'w_mod': nrm(ks[4], (DEPTH, D_MODEL, 3 * D_MODEL), 0.5 * D_MODEL ** -0.5),
        'b_mod': nrm(ks[5], (DEPTH, 3 * D_MODEL), 0.01),
        'norm_g': 1.0 + nrm(ks[6], (DEPTH, D_MODEL), 0.05),
        'ev_w_in': nrm(ks[7], (N_EVEN, D_MODEL, EVEN_IN), D_MODEL ** -0.5),
        'ev_w_out': nrm(ks[8], (N_EVEN, EVEN_MIX, D_MODEL), EVEN_MIX ** -0.5),
        'pool_w': nrm(ks[9], (N_EVEN, POOL_GROUPS, POOL_GROUP_DIM, POOL_GROUP_DIM), POOL_GROUP_DIM ** -0.5),
        'pool_scale': 1.0 + nrm(ks[10], (N_EVEN, POOL_WIDTH), 0.1),
        'q_norm_g': 1.0 + nrm(ks[11], (N_EVEN, HEAD_DIM), 0.05),
        'k_norm_g': 1.0 + nrm(ks[12], (N_EVEN, HEAD_DIM), 0.05),
        'od_w_in': nrm(ks[13], (N_ODD, D_MODEL, ODD_IN), D_MODEL ** -0.5),
        'od_w_out': nrm(ks[14], (N_ODD, ODD_MIX, D_MODEL), ODD_MIX ** -0.5),
        'hy_conv_w': nrm(ks[15], (N_ODD, 3, (HY_ORDER + 1) * HY_WIDTH), 3 ** -0.5),
        'hy_conv_b': nrm(ks[16], (N_ODD, (HY_ORDER + 1) * HY_WIDTH), 0.01),
        'hy_w1': nrm(ks[17], (N_ODD, HY_EMB, HY_HIDDEN), HY_EMB ** -0.5),
        'hy_b1': nrm(ks[18], (N_ODD, HY_HIDDEN), 0.1),
        'hy_w2': nrm(ks[19], (N_ODD, HY_HIDDEN, HY_HIDDEN), HY_HIDDEN ** -0.5),
        'hy_b2': nrm(ks[20], (N_ODD, HY_HIDDEN), 0.1),
        'hy_w3': nrm(ks[21], (N_ODD, HY_HIDDEN, HY_ORDER * 2 * HY_WIDTH), HY_HIDDEN ** -0.5),
        'hy_freq': 1.0 + nrm(ks[22], (N_ODD, HY_HIDDEN), 0.1),
        'hy_skip': nrm(ks[23], (N_ODD, HY_ORDER, HY_WIDTH), 1.0),
        'fn_w': nrm(ks[24], (N_ODD, FN_WIDTH, FN_WIDTH), FN_WIDTH ** -0.5),
        'final_g': 1.0 + nrm(ks[25], (D_MODEL,), 0.05),
    }


def reference(x, c, ctx, c_ctx, w_mod, b_mod, norm_g, ev_w_in, ev_w_out, pool_w, pool_scale,
              q_norm_g, k_norm_g, od_w_in, od_w_out, hy_conv_w, hy_conv_b, hy_w1, hy_b1, hy_w2, hy_b2,
              hy_w3, hy_freq, hy_skip, fn_w, final_g):
    n_tok = x.shape[1]
    rows = n_tok // GRID_W
    row_idx = jnp.broadcast_to(jnp.arange(rows, dtype=jnp.float32)[:, None], (rows, GRID_W)).reshape(-1)
    col_idx = jnp.broadcast_to(jnp.arange(GRID_W, dtype=jnp.float32)[None, :], (rows, GRID_W)).reshape(-1)
    cos, sin = axial_rope_tables(row_idx, col_idx)

    for i in range(DEPTH):
        li = i // 2
        is_even = i % 2 == 0
        ctx_needed = any(j % 2 == 0 for j in range(i + 1, DEPTH))
        shift, scale, gate = modulation(c[:, None, :], w_mod[i], b_mod[i])
        h = rmsnorm(x, norm_g[i]) * (1 + scale) + shift
        if is_even or ctx_needed:
            cshift, cscale, cgate = modulation(c_ctx[None, None, :], w_mod[i], b_mod[i])
            hc = rmsnorm(ctx, norm_g[i]) * (1 + cscale) + cshift
        if is_even:
            w_in = ev_w_in[li]
            a_val, a_gate, q, k, v, b_gate = jnp.split(h @ w_in, EVEN_SPLITS, axis=-1)
            q = apply_rope(rmsnorm(to_heads(q, N_Q_HEADS), q_norm_g[li]), cos, sin)
            k = apply_rope(rmsnorm(to_heads(k, N_KV_HEADS), k_norm_g[li]), cos, sin)
            if ctx_needed:
                ca_val, ca_gate, cq, ck, cv, cb_gate = jnp.split(hc @ w_in, EVEN_SPLITS, axis=-1)
            else:
                ck, cv = jnp.split(hc @ w_in[:, EVEN_SPLITS[2]:EVEN_SPLITS[4]], 2, axis=-1)
            ck = rmsnorm(to_heads(ck, N_KV_HEADS), k_norm_g[li])
            cv = to_heads(cv, N_KV_HEADS)
            k_all = jnp.concatenate([ck, k], axis=1)
            v_all = jnp.concatenate([cv, to_heads(v, N_KV_HEADS)], axis=1)
            y = even_mix(a_val, a_gate, q, b_gate, k_all, v_all, pool_w[li], pool_scale[li], ev_w_out[li])
            if ctx_needed:
                cq = rmsnorm(to_heads(cq, N_Q_HEADS), q_norm_g[li])
                yc = even_mix(ca_val, ca_gate, cq, cb_gate, ck, cv, pool_w[li], pool_scale[li], ev_w_out[li])
                ctx = ctx + cgate * yc
            x = x + gate * y
        else:
            y = odd_mix(h, od_w_in[li], od_w_out[li], fn_w[li], hy_conv_w[li], hy_conv_b[li],
                        hy_w1[li], hy_b1[li], hy_w2[li], hy_b2[li], hy_w3[li], hy_freq[li], hy_skip[li])
            if ctx_needed:
                yc = odd_mix(hc, od_w_in[li], od_w_out[li], fn_w[li], hy_conv_w[li], hy_conv_b[li],
                             hy_w1[li], hy_b1[li], hy_w2[li], hy_b2[li], hy_w3[li], hy_freq[li], hy_skip[li])
                ctx = ctx + cgate * yc
            x = x + gate * y
    return rmsnorm(x, final_g)
```

```python
import ml_dtypes
import numpy as np
from contextlib import ExitStack
import concourse.bass as bass
import concourse.mybir as mybir
from concourse.bass_utils import run_bass_kernel_spmd

F32 = mybir.dt.float32
BF16 = mybir.dt.bfloat16
AF = mybir.ActivationFunctionType
ALU = mybir.AluOpType
AX = mybir.AxisListType


class Prog:
    DMA_SLOTS = 8

    def __init__(self, nc, es):
        self.nc = nc
        self.es = es
        self.ops = []
        self.last_w = {}
        self.readers = {}
        self.n_names = 0
        self.excl = set()

    def sb(self, name, shape, dt):
        return self.es.enter_context(self.nc.sbuf_tensor(name, list(shape), dt))

    def ps(self, name, shape, dt=F32):
        self.excl.add(name)
        return self.es.enter_context(self.nc.psum_tensor(name, list(shape), dt))

    def op(self, eng, fn, r=(), w=(), dma=False):
        idx = len(self.ops)
        deps = {}
        w = list(w) + [x for x in r if x in self.excl]
        r = [x for x in r if x not in self.excl]
        for res in r:
            d = self.last_w.get(res)
            if d is not None:
                deps[d] = True
        for res in w:
            d = self.last_w.get(res)
            if d is not None:
                deps.setdefault(d, False)
            for rd in self.readers.get(res, ()):
                deps.setdefault(rd, False)
        for res in r:
            self.readers.setdefault(res, []).append(idx)
        for res in w:
            self.last_w[res] = idx
            self.readers[res] = []
        deps.pop(idx, None)
        self.ops.append(dict(eng=eng, fn=fn, deps=deps, dma=dma))
        return idx

    def dma(self, q, out, in_, r=(), w=(), **kw):
        return self.op(q, lambda e: e.dma_start(out=out, in_=in_, **kw), r, w, dma=True)

    def emit(self):
        nc = self.nc
        ops = self.ops
        n = len(ops)
        engs = ['tensor', 'vector', 'scalar', 'gpsimd', 'sync']
        need = [False] * n
        for i, o in enumerate(ops):
            keep = {}
            for d, raw in o['deps'].items():
                pd = ops[d]
                if (not pd['dma']) and (not o['dma']) and pd['eng'] == o['eng']:
                    if pd['eng'] == 'tensor' or not raw:
                        continue
                keep[d] = raw
                if not pd['dma']:
                    need[d] = True
            o['deps'] = keep
        esem = {e: self.es.enter_context(nc.semaphore("sem_" + e)) for e in engs}
        dsem = {e: [self.es.enter_context(nc.semaphore("dsem_%s_%d" % (e, k))) for k in range(self.DMA_SLOTS)]
                for e in engs}
        cnt = {e: 0 for e in engs}
        dk = {e: 0 for e in engs}
        dcnt = {e: [0] * self.DMA_SLOTS for e in engs}
        ticket = [None] * n
        prew = [None] * n
        for i, o in enumerate(ops):
            e = o['eng']
            if o['dma']:
                s = dk[e] % self.DMA_SLOTS
                dk[e] += 1
                if dcnt[e][s] > 0:
                    prew[i] = (dsem[e][s], dcnt[e][s])
                dcnt[e][s] += 16
                ticket[i] = (dsem[e][s], dcnt[e][s])
            elif need[i]:
                cnt[e] += 1
                ticket[i] = (esem[e], cnt[e])
        with nc.Block() as block:
            def section(ename):
                def body(eng):
                    waited = {}
                    for i, o in enumerate(ops):
                        if o['eng'] != ename:
                            continue
                        waits = {}
                        for d in o['deps']:
                            sem, val = ticket[d]
                            if waits.get(sem, (None, 0))[1] < val:
                                waits[sem] = (sem, val)
                        if prew[i] is not None:
                            sem, val = prew[i]
                            if waits.get(sem, (None, 0))[1] < val:
                                waits[sem] = (sem, val)
                        for sem, val in waits.values():
                            if waited.get(sem, 0) < val:
                                eng.wait_ge(sem, val)
                                waited[sem] = val
                        if o['fn'] is None:
                            continue
                        ins = o['fn'](eng)
                        if ticket[i] is not None:
                            ins.then_inc(ticket[i][0], 16 if o['dma'] else 1)
                return body
            block.tensor(section('tensor'))
            block.vector(section('vector'))
            block.scalar(section('scalar'))
            block.gpsimd(section('gpsimd'))
            block.sync(section('sync'))


BF = ml_dtypes.bfloat16
EPS = 1e-6
NT = 2048


def new_nc():
    return bass.Bass("TRN2", target_bir_lowering=False)


def build_p1(stage=99):
    nc = new_nc()
    def din(name, shape, dt=F32): return nc.dram_tensor(name, list(shape), dt, kind="ExternalInput").ap()
    def dout(name, shape, dt=F32): return nc.dram_tensor(name, list(shape), dt, kind="ExternalOutput").ap()
    x = din("x", [2048, 1024]); xh = din("xh", [16, 1024]); ctx = din("ctx", [256, 1024])
    ccT = din("ccT", [128, 16]); w_mod = din("w_mod", [1024, 3072]); b_mod = din("b_mod", [1, 3072]); gT = din("gT", [128, 8])
    w_in = din("w_in", [1024, 2560]); pool_bd = din("pool_bd", [2, 128, 128]); psT = din("psT", [128, 2]); qkg = din("qkg", [128, 2])
    idb = din("idb", [128, 128], BF16); id2 = din("id2", [2, 2]); ones = din("ones", [128, 128]); RT = din("RT", [128, 128])
    cos = din("cos", [128, 2048]); sin = din("sin", [128, 2048]); invw = din("invw", [128, 2]); corr = din("corr", [128, 2, 16]); hmask = din("hmask", [128, 16])
    QT = dout("QT", [6, 128, 2048], BF16); KT = dout("KT", [2, 128, 2048], BF16); V = dout("V", [2048, 256], BF16)
    cKT = dout("cKT", [2, 128, 256], BF16); cV = dout("cV", [256, 256], BF16)
    yaT = dout("yaT", [2, 128, 2048], BF16); sbg = dout("sbg", [2048, 768], BF16); gate = dout("gate", [1, 1024])
    with ExitStack() as es:
        P = Prog(nc, es)
        ids = P.sb("ids", [128, 128], BF16); id2s = P.sb("id2s", [2, 2], F32); oness = P.sb("oness", [128, 128], F32); RTs = P.sb("RTs", [128, 128], F32)
        gTs = P.sb("gTs", [128, 8], F32); psTs = P.sb("psTs", [128, 2], F32); qkgs = P.sb("qkgs", [128, 2], F32)
        invws = P.sb("invws", [128, 2], F32); corrs = P.sb("corrs", [128, 2, 16], F32); hmasks = P.sb("hmasks", [128, 16], F32)
        epsT = P.sb("epsT", [128, 1], F32); ccs = P.sb("ccs", [128, 16], F32); scc = P.sb("scc", [128, 16], F32)
        bms = P.sb("bms", [2, 3072], F32); msb = P.sb("msb", [2, 3072], F32); modT = P.sb("modT", [128, 24, 2], F32); gs = P.sb("gs", [128, 8, 2], F32)
        pbd = P.sb("pbd", [128, 2, 128], BF16)
        for (t, src, nm) in [(ids, idb, 'ids'), (id2s, id2, 'id2'), (oness, ones, 'ones'), (RTs, RT, 'RT'), (gTs, gT, 'gT'), (psTs, psT, 'psT'),
                             (qkgs, qkg, 'qkg'), (invws, invw, 'invw'), (corrs, corr, 'corr'), (hmasks, hmask, 'hmask'), (ccs, ccT, 'cc')]:
            P.dma('sync', t[:], src, w=[nm])
        P.dma('sync', bms[:], b_mod.partition_broadcast(2), w=['bm'])
        P.dma('gpsimd', pbd[:], pool_bd.rearrange("c p n -> p c n"), w=['pbd'])
        P.op('vector', lambda e: e.memset(epsT[:], EPS), w=['eps'])
        P.op('scalar', lambda e: e.activation(out=scc[:], in_=ccs[:], func=AF.Silu), r=['cc'], w=['scc'])
        wm = [P.sb("wm%d" % i, [128, 8, 128], F32) for i in range(2)]
        pn = P.ps("pn", [128, 512], F32); pr = P.ps("pr", [128, 512], F32)
        pm = pn
        wmv = w_mod.rearrange("(kc p) n -> p kc n", p=128)
        for j in range(24):
            b = j % 2
            P.dma('sync', wm[b][:], wmv[:, :, j * 128:(j + 1) * 128], w=['wm%d' % b])
            def mm(e, j=j, b=b):
                for kc in range(8):
                    ins = e.matmul(pm[0:2, (j % 4) * 128:(j % 4 + 1) * 128], lhsT=scc[:, kc * 2:kc * 2 + 2], rhs=wm[b][:, kc, :], start=(kc == 0), stop=(kc == 7))
                return ins
            P.op('tensor', mm, r=['scc', 'wm%d' % b], w=['pn'])
            P.op('vector', lambda e, j=j: e.tensor_tensor(out=msb[:, j * 128:(j + 1) * 128], in0=pm[0:2, (j % 4) * 128:(j % 4 + 1) * 128], in1=bms[:, j * 128:(j + 1) * 128], op=ALU.add),
                 r=['pn', 'bm'], w=['msb%d' % j])
        pT = pr
        def tr(e):
            for j in range(24):
                ins = e.matmul(pT[:, 2 * j:2 * j + 2], lhsT=msb[:, j * 128:(j + 1) * 128], rhs=id2s[:], start=True, stop=True)
            return ins
        P.op('tensor', tr, r=['msb%d' % j for j in range(24)] + ['id2'], w=['pr'])
        P.op('vector', lambda e: e.tensor_copy(out=modT[:].rearrange("p a b -> p (a b)"), in_=pT[:, 0:48]), r=['pr'], w=['modT'])
        for r_ in range(2):
            P.op('vector', lambda e, r_=r_: e.scalar_tensor_tensor(out=gs[:, :, r_], in0=modT[:, 8:16, r_], scalar=1.0, in1=gTs[:], op0=ALU.add, op1=ALU.mult),
                 r=['modT', 'gT'], w=['gs%d' % r_])
        P.dma('sync', gate, msb[0:1, 2048:3072], r=['msb%d' % j for j in range(16, 24)], w=['o_gate'])
        if stage < 1:
            P.op('sync', None, r=[k for k in P.last_w if k.startswith('o_')]); P.emit(); return nc
        wi = P.sb("wi", [128, 8, 2560], BF16)
        wiv = w_in.rearrange("(kc p) n -> p kc n", p=128)
        for kc in range(8):
            if stage == 1: continue
            P.dma('gpsimd', wi[:, kc, :], wiv[:, kc, :], w=['wi%d' % kc])
        WI = ['wi%d' % kc for kc in range(8)]
        hT = P.sb("hT", [128, 8, 2064], BF16); hcT = P.sb("hcT", [128, 8, 256], BF16)
        xt = [P.sb("xt%d" % i, [128, 1024], F32) for i in range(2)]
        xn = [P.sb("xn%d" % i, [128, 1024], BF16) for i in range(2)]
        junk = P.sb("junk", [128, 1024], BF16)
        ss = [P.sb("ss%d" % i, [128, 2], F32) for i in range(2)]
        ptr = [P.ps("ptr%d" % i, [128, 8, 128], BF16) for i in range(2)]
        tiles = [(x[i * 128:(i + 1) * 128, :], 128, hT, i * 128, 0, 'hT%d' % i) for i in range(16)]
        tiles.append((xh, 16, hT, 2048, 0, 'hT16'))
        tiles += [(ctx[i * 128:(i + 1) * 128, :], 128, hcT, i * 128, 1, 'hcT%d' % i) for i in range(2)]
        for ti, (src, n, dst, c0, r_, res) in enumerate(tiles):
            b = ti % 2
            P.dma('sync', xt[b][0:n, :], src, w=['xt%d' % b])
            P.op('scalar', lambda e, b=b, n=n: e.activation(out=junk[0:n, :], in_=xt[b][0:n, :], func=AF.Square, accum_out=ss[b][0:n, 0:1]), r=['xt%d' % b], w=['junk', 'ssa%d' % b])
            P.op('scalar', lambda e, b=b, n=n: e.activation(out=ss[b][0:n, 1:2], in_=ss[b][0:n, 0:1], func=AF.Sqrt, scale=1.0 / 1024, bias=epsT[0:n, :]), r=['ssa%d' % b, 'eps'], w=['ssb%d' % b])
            P.op('vector', lambda e, b=b, n=n: e.reciprocal(out=ss[b][0:n, 1:2], in_=ss[b][0:n, 1:2]), r=['ssb%d' % b], w=['ssb%d' % b])
            P.op('vector', lambda e, b=b, n=n: e.tensor_scalar(out=xn[b][0:n, :], in0=xt[b][0:n, :], scalar1=ss[b][0:n, 1:2], scalar2=None, op0=ALU.mult), r=['xt%d' % b, 'ssb%d' % b], w=['xn%d' % b])
            def trs(e, b=b, n=n):
                for kc in range(8):
                    ins = e.transpose(ptr[b][:, kc, 0:n], xn[b][0:n, kc * 128:(kc + 1) * 128], ids[0:n, 0:n])
                return ins
            P.op('tensor', trs, r=['xn%d' % b, 'ids'], w=['ptr%d' % b])
            def ev_v(e, b=b, n=n, dst=dst, c0=c0, r_=r_):
                for kc in range(0, 8, 2):
                    ins = e.tensor_scalar(out=dst[:, kc, c0:c0 + n], in0=ptr[b][:, kc, 0:n], scalar1=gs[:, kc, r_:r_ + 1], scalar2=modT[:, kc, r_:r_ + 1], op0=ALU.mult, op1=ALU.add)
                return ins
            def ev_s(e, b=b, n=n, dst=dst, c0=c0, r_=r_):
                for kc in range(1, 8, 2):
                    ins = e.activation(out=dst[:, kc, c0:c0 + n], in_=ptr[b][:, kc, 0:n], func=AF.Identity, scale=gs[:, kc, r_:r_ + 1], bias=modT[:, kc, r_:r_ + 1])
                return ins
            if ti % 2 == 0:
                def ev(e, b=b, n=n, dst=dst, c0=c0, r_=r_):
                    for kc in range(8):
                        ins = e.tensor_scalar(out=dst[:, kc, c0:c0 + n], in0=ptr[b][:, kc, 0:n], scalar1=gs[:, kc, r_:r_ + 1], scalar2=modT[:, kc, r_:r_ + 1], op0=ALU.mult, op1=ALU.add)
                    return ins
                P.op('vector', ev, r=['ptr%d' % b, 'gs%d' % r_, 'modT'], w=[res + 'a', res + 'b'])
            else:
                def ev(e, b=b, n=n, dst=dst, c0=c0, r_=r_):
                    for kc in range(8):
                        ins = e.activation(out=dst[:, kc, c0:c0 + n], in_=ptr[b][:, kc, 0:n], func=AF.Identity, scale=gs[:, kc, r_:r_ + 1], bias=modT[:, kc, r_:r_ + 1])
                    return ins
                P.op('scalar', ev, r=['ptr%d' % b, 'gs%d' % r_, 'modT'], w=[res + 'a', res + 'b'])
        def HT(i): return ['hT%da' % i, 'hT%db' % i]
        if stage < 2:
            P.op('sync', None, r=[k for k in P.last_w if k.startswith('o_')]); P.emit(); return nc
        pt = P.ps("pt", [128, 1024], F32)
        vst = [P.sb("vst%d" % i, [128, 1024], BF16) for i in range(2)]
        for tt in range(16):
            b = tt % 2
            def mm(e, tt=tt):
                for half in range(2):
                    for kc in range(8):
                        ins = e.matmul(pt[:, half * 512:(half + 1) * 512], lhsT=hT[:, kc, tt * 128:(tt + 1) * 128], rhs=wi[:, kc, 1536 + half * 512:1536 + (half + 1) * 512], start=(kc == 0), stop=(kc == 7))
                return ins
            P.op('tensor', mm, r=HT(tt) + WI, w=['pt'])
            P.op('vector', lambda e, b=b: e.tensor_copy(out=vst[b][:, 0:256], in_=pt[:, 0:256]), r=['pt'], w=['vst%da' % b])
            P.op('scalar', lambda e, b=b: e.activation(out=vst[b][:, 256:1024], in_=pt[:, 256:1024], func=AF.Silu), r=['pt'], w=['vst%db' % b])
            P.dma('sync', V[tt * 128:(tt + 1) * 128, :], vst[b][:, 0:256], r=['vst%da' % b], w=['o_V%d' % tt])
            P.dma('sync', sbg[tt * 128:(tt + 1) * 128, :], vst[b][:, 256:1024], r=['vst%db' % b], w=['o_sbg%d' % tt])
        for tt in range(2):
            b = tt % 2
            def mm(e, tt=tt):
                for kc in range(8):
                    ins = e.matmul(pt[:, 0:256], lhsT=hcT[:, kc, tt * 128:(tt + 1) * 128], rhs=wi[:, kc, 1536:1792], start=(kc == 0), stop=(kc == 7))
                return ins
            P.op('tensor', mm, r=['hcT%da' % tt, 'hcT%db' % tt] + WI, w=['pt'])
            P.op('vector', lambda e, b=b: e.tensor_copy(out=vst[b][:, 0:256], in_=pt[:, 0:256]), r=['pt'], w=['vst%da' % b])
            P.dma('sync', cV[tt * 128:(tt + 1) * 128, :], vst[b][:, 0:256], r=['vst%da' % b], w=['o_cV%d' % tt])
        if stage < 3:
            P.op('sync', None, r=[k for k in P.last_w if k.startswith('o_')]); P.emit(); return nc
        pf = [P.ps("pf%d" % i, [128, 512], F32) for i in range(2)]
        av = P.sb("av", [128, 2, 2064], F32); sag = P.sb("sag", [128, 2, 2048], BF16)
        qraw = P.sb("qraw", [128, 512], F32); sq = P.sb("sq", [128, 512], F32); rstd = P.sb("rstd", [128, 512], F32)
        qn = P.sb("qn", [128, 512], F32); t1 = P.sb("t1", [128, 512], F32); t2 = P.sb("t2", [128, 512], F32)
        qst = [P.sb("qst%d" % i, [128, 512], BF16) for i in range(2)]
        cst = [P.sb("cst%d" % i, [128, 512], F32) for i in range(2)]; snt = [P.sb("snt%d" % i, [128, 512], F32) for i in range(2)]
        jobn = [0]
        def proj(colchunk, rhsT, c0, n, rres):
            b = jobn[0] % 2
            jobn[0] += 1
            def mm(e):
                for kc in range(8):
                    ins = e.matmul(pf[b][:, 0:n], lhsT=wi[:, kc, colchunk * 128:(colchunk + 1) * 128], rhs=rhsT[:, kc, c0:c0 + n], start=(kc == 0), stop=(kc == 7))
                return ins
            P.op('tensor', mm, r=rres + WI, w=['pf%d' % b])
            return b
        def qknorm(b, n, gcol, rope_tb, out_ap, out_res):
            P.op('scalar', lambda e: e.activation(out=qraw[:, 0:n], in_=pf[b][:, 0:n], func=AF.Copy), r=['pf%d' % b], w=['qraw'])
            P.op('vector', lambda e: e.tensor_tensor(out=sq[:, 0:n], in0=qraw[:, 0:n], in1=qraw[:, 0:n], op=ALU.mult), r=['qraw'], w=['sq'])
            P.op('tensor', lambda e: e.matmul(pn[:, 0:n], lhsT=oness[:], rhs=sq[:, 0:n], start=True, stop=True), r=['sq', 'ones'], w=['pn'])
            P.op('scalar', lambda e: e.activation(out=rstd[:, 0:n], in_=pn[:, 0:n], func=AF.Sqrt, scale=1.0 / 128, bias=epsT[:]), r=['pn', 'eps'], w=['rstd'])
            P.op('vector', lambda e: e.reciprocal(out=rstd[:, 0:n], in_=rstd[:, 0:n]), r=['rstd'], w=['rstd'])
            if rope_tb is None:
                P.op('vector', lambda e: e.scalar_tensor_tensor(out=out_ap, in0=qraw[:, 0:n], scalar=qkgs[:, gcol:gcol + 1], in1=rstd[:, 0:n], op0=ALU.mult, op1=ALU.mult),
                     r=['qraw', 'rstd', 'qkg'], w=[out_res])
                return
            cb = rope_tb % 2
            P.op('vector', lambda e: e.scalar_tensor_tensor(out=qn[:, 0:n], in0=qraw[:, 0:n], scalar=qkgs[:, gcol:gcol + 1], in1=rstd[:, 0:n], op0=ALU.mult, op1=ALU.mult),
                 r=['qraw', 'rstd', 'qkg'], w=['qn'])
            P.op('tensor', lambda e: e.matmul(pr[:, 0:n], lhsT=RTs[:], rhs=qn[:, 0:n], start=True, stop=True), r=['qn', 'RT'], w=['pr'])
            P.op('gpsimd', lambda e: e.tensor_tensor(out=t1[:, 0:n], in0=qn[:, 0:n], in1=cst[cb][:, 0:n], op=ALU.mult), r=['qn', 'cst%d' % cb], w=['t1'])
            P.op('vector', lambda e: e.tensor_tensor(out=t2[:, 0:n], in0=pr[:, 0:n], in1=snt[cb][:, 0:n], op=ALU.mult), r=['pr', 'snt%d' % cb], w=['t2'])
            P.op('vector', lambda e: e.tensor_tensor(out=out_ap, in0=t1[:, 0:n], in1=t2[:, 0:n], op=ALU.add), r=['t1', 't2'], w=[out_res])
        for hd in range(2):
            b = proj(10 + hd, hcT, 0, 256, ['hcT0a', 'hcT0b', 'hcT1a', 'hcT1b'])
            qb = jobn[0] % 2
            qknorm(b, 256, 1, None, qst[qb][:, 0:256], 'qst%d' % qb)
            P.dma('sync', cKT[hd], qst[qb][:, 0:256], r=['qst%d' % qb], w=['o_cKT%d' % hd])
        for c in range(2):
            b = proj(c, hT, 2048, 16, HT(16))
            P.op('vector', lambda e, b=b, c=c: e.tensor_tensor(out=av[:, c, 0:8], in0=pf[b][:, 0:8], in1=hmasks[:, 0:8], op=ALU.mult), r=['pf%d' % b, 'hmask'], w=['avh%da' % c])
            P.op('vector', lambda e, b=b, c=c: e.tensor_tensor(out=av[:, c, 2056:2064], in0=pf[b][:, 8:16], in1=hmasks[:, 8:16], op=ALU.mult), r=['pf%d' % b, 'hmask'], w=['avh%db' % c])
        for tb in range(4):
            cb = tb % 2
            P.dma('sync', cst[cb][:], cos[:, tb * 512:(tb + 1) * 512], w=['cst%d' % cb])
            P.dma('sync', snt[cb][:], sin[:, tb * 512:(tb + 1) * 512], w=['snt%d' % cb])
            hres = [r_ for i in range(tb * 4, tb * 4 + 4) for r_ in HT(i)]
            for c in range(2):
                b = proj(c, hT, tb * 512, 512, hres)
                P.op('scalar', lambda e, b=b, c=c, tb=tb: e.activation(out=av[:, c, 8 + tb * 512:8 + (tb + 1) * 512], in_=pf[b][:], func=AF.Copy), r=['pf%d' % b], w=['av%d_%d' % (c, tb)])
                b = proj(2 + c, hT, tb * 512, 512, hres)
                P.op('scalar', lambda e, b=b, c=c, tb=tb: e.activation(out=sag[:, c, tb * 512:(tb + 1) * 512], in_=pf[b][:], func=AF.Silu), r=['pf%d' % b], w=['sag%d_%d' % (c, tb)])
            for hd in range(8):
                b = proj(4 + hd, hT, tb * 512, 512, hres)
                qb = jobn[0] % 2
                qknorm(b, 512, 0 if hd < 6 else 1, tb, qst[qb][:], 'qst%d' % qb)
                dst = QT[hd, :, tb * 512:(tb + 1) * 512] if hd < 6 else KT[hd - 6, :, tb * 512:(tb + 1) * 512]
                P.dma('sync', dst, qst[qb][:], r=['qst%d' % qb], w=['o_q%d_%d' % (hd, tb)])
        if stage < 4:
            P.op('sync', None, r=[k for k in P.last_w if k.startswith('o_')]); P.emit(); return nc
        A = P.sb("pA", [128, 2064], F32); B = P.sb("pB", [128, 2064], F32); C = P.sb("pC", [128, 2064], F32)
        dd = P.sb("dd", [128, 2048], BF16)
        pp = pf
        for c in range(2):
            avres = ['av%d_%d' % (c, tb) for tb in range(4)] + ['avh%da' % c, 'avh%db' % c]
            u = av[:, c, :]
            P.op('vector', lambda e, u=u: e.tensor_tensor(out=A[:, 1:2064], in0=u[:, 1:2064], in1=u[:, 0:2063], op=ALU.add), r=avres, w=['pA'])
            P.op('vector', lambda e: e.tensor_tensor(out=B[:, 3:2064], in0=A[:, 3:2064], in1=A[:, 1:2062], op=ALU.add), r=['pA'], w=['pB'])
            if c == 0:
                P.op('vector', lambda e: e.tensor_scalar(out=C[0:64, 0:2048], in0=A[0:64, 8:2056], scalar1=invws[0:64, 0:1], scalar2=None, op0=ALU.mult), r=['pA', 'invw'], w=['pCa'])
                P.op('vector', lambda e: e.tensor_scalar(out=C[64:128, 0:2048], in0=B[64:128, 9:2057], scalar1=invws[64:128, 0:1], scalar2=None, op0=ALU.mult), r=['pB', 'invw'], w=['pCb'])
                tmp = C; tres = ['pCa', 'pCb']
            else:
                P.op('vector', lambda e: e.tensor_tensor(out=A[:, 7:2064], in0=B[:, 7:2064], in1=B[:, 3:2060], op=ALU.add), r=['pB'], w=['pA'])
                P.op('vector', lambda e: e.tensor_tensor(out=C[:, 15:2064], in0=A[:, 15:2064], in1=A[:, 7:2056], op=ALU.add), r=['pA'], w=['pCa', 'pCb'])
                P.op('vector', lambda e: e.tensor_scalar(out=B[0:64, 0:2048], in0=A[0:64, 11:2059], scalar1=invws[0:64, 1:2], scalar2=None, op0=ALU.mult), r=['pA', 'invw'], w=['pBa'])
                P.op('vector', lambda e: e.tensor_scalar(out=B[64:128, 0:2048], in0=C[64:128, 15:2063], scalar1=invws[64:128, 1:2], scalar2=None, op0=ALU.mult), r=['pCa', 'pCb', 'invw'], w=['pBb'])
                tmp = B; tres = ['pBa', 'pBb', 'pB']
            P.op('vector', lambda e, tmp=tmp, c=c: e.tensor_tensor(out=tmp[:, 0:8], in0=tmp[:, 0:8], in1=corrs[:, c, 0:8], op=ALU.mult), r=tres + ['corr'], w=['tmpL'])
            P.op('vector', lambda e, tmp=tmp, c=c: e.tensor_tensor(out=tmp[:, 2040:2048], in0=tmp[:, 2040:2048], in1=corrs[:, c, 8:16], op=ALU.mult), r=tres + ['corr'], w=['tmpR'])
            P.op('vector', lambda e, tmp=tmp, u=u: e.tensor_tensor(out=dd[:], in0=tmp[:, 0:2048], in1=u[:, 8:2056], op=ALU.subtract), r=tres + ['tmpL', 'tmpR'] + avres, w=['dd'] + tres)
            for tb in range(4):
                b = jobn[0] % 2
                jobn[0] += 1
                P.op('tensor', lambda e, b=b, c=c, tb=tb: e.matmul(pp[b][:], lhsT=pbd[:, c, :], rhs=dd[:, tb * 512:(tb + 1) * 512], start=True, stop=True), r=['dd', 'pbd'], w=['pf%d' % b])
                P.op('vector', lambda e, b=b, c=c, tb=tb: e.scalar_tensor_tensor(out=qst[b][:], in0=pp[b][:], scalar=psTs[:, c:c + 1], in1=sag[:, c, tb * 512:(tb + 1) * 512], op0=ALU.mult, op1=ALU.mult),
                     r=['pf%d' % b, 'psT', 'sag%d_%d' % (c, tb)], w=['qst%d' % b])
                P.dma('sync', yaT[c, :, tb * 512:(tb + 1) * 512], qst[b][:], r=['qst%d' % b], w=['o_ya%d_%d' % (c, tb)])
        P.op('sync', None, r=[k for k in P.last_w if k.startswith('o_')])
        P.emit()
    return nc


def fm(v):
    return np.ascontiguousarray(np.asarray(v, np.float32).reshape(8, 128).T)


def p1_inputs(I):
    x = I['x'][0]
    L = x.shape[0]
    consts = {}
    consts['idb'] = np.eye(128, dtype=np.float32).astype(BF)
    consts['id2'] = np.eye(2, dtype=np.float32)
    consts['ones'] = np.ones((128, 128), np.float32)
    Rm = np.zeros((128, 128), np.float32)
    for a in range(2):
        for f in range(32):
            Rm[a * 64 + f, a * 64 + 32 + f] = -1.0
            Rm[a * 64 + 32 + f, a * 64 + f] = 1.0
    consts['RT'] = np.ascontiguousarray(Rm.T)
    inv_freq = (10000.0 ** (-np.arange(32, dtype=np.float32) / 32)).astype(np.float32)
    t = np.arange(L)
    row = (t // 64).astype(np.float32); col = (t % 64).astype(np.float32)
    ang = np.zeros((128, L), np.float32)
    for a, idx in enumerate((row, col)):
        for h in range(2):
            ang[a * 64 + h * 32:a * 64 + h * 32 + 32, :] = inv_freq[:, None] * idx[None, :]
    cosT = np.cos(ang).astype(np.float32); sinT = np.sin(ang).astype(np.float32)
    wins = [2, 4, 8, 16]
    invw = np.zeros((128, 2), np.float32)
    for c in range(2):
        invw[0:64, c] = 1.0 / wins[2 * c]; invw[64:128, c] = 1.0 / wins[2 * c + 1]
    cc = np.stack([fm(I['c'][0]), fm(I['c_ctx'])], axis=-1).reshape(128, 16)
    pool_bd = np.zeros((2, 128, 128), np.float32)
    for g in range(4):
        pool_bd[g // 2, (g % 2) * 64:(g % 2) * 64 + 64, (g % 2) * 64:(g % 2) * 64 + 64] = I['pool_w'][0, g]
    shared = dict(consts, ctx=np.ascontiguousarray(I['ctx'][0]), ccT=np.ascontiguousarray(cc), w_mod=np.ascontiguousarray(I['w_mod'][0]),
                  b_mod=np.ascontiguousarray(I['b_mod'][0:1]), gT=fm(I['norm_g'][0]), w_in=np.ascontiguousarray(I['ev_w_in'][0]), pool_bd=pool_bd,
                  psT=np.ascontiguousarray(I['pool_scale'][0].reshape(2, 128).T), qkg=np.ascontiguousarray(np.stack([I['q_norm_g'][0], I['k_norm_g'][0]], axis=1)), invw=invw)
    maps = []
    for i in range(8):
        t0 = i * NT
        xh = np.zeros((16, 1024), np.float32); hm = np.zeros((128, 16), np.float32)
        if i > 0:
            xh[0:8] = x[t0 - 8:t0]; hm[:, 0:8] = 1.0
        if i < 7:
            xh[8:16] = x[t0 + NT:t0 + NT + 8]; hm[:, 8:16] = 1.0
        corr = np.ones((128, 2, 16), np.float32)
        for c in range(2):
            for hh in range(2):
                w = wins[2 * c + hh]
                for j in range(8):
                    for side, tt in ((0, t0 + j), (1, t0 + NT - 8 + j)):
                        lo = min(max(tt - w // 2, 0), L); hi = min(max(tt + w // 2, 0), L)
                        corr[hh * 64:(hh + 1) * 64, c, side * 8 + j] = w / float(hi - lo)
        m = dict(shared, x=np.ascontiguousarray(x[t0:t0 + NT]), xh=xh, hmask=hm, corr=corr,
                 cos=np.ascontiguousarray(cosT[:, t0:t0 + NT]), sin=np.ascontiguousarray(sinT[:, t0:t0 + NT]))
        maps.append(m)
    return maps


BF = ml_dtypes.bfloat16
NKC = 130


def build_p2(n_kc=NKC, qbs=4):
    nc = bass.Bass("TRN2", target_bir_lowering=False)
    def din(name, shape, dt=F32): return nc.dram_tensor(name, list(shape), dt, kind="ExternalInput").ap()
    def dout(name, shape, dt=F32): return nc.dram_tensor(name, list(shape), dt, kind="ExternalOutput").ap()
    QT = din("QT", [6, 128, 2048], BF16); KTa = din("KTa", [2, 128, NKC * 128], BF16); Va = din("Va", [2, 128, NKC * 129], BF16)
    yaT = din("yaT", [2, 128, 2048], BF16); sbg = din("sbg", [2048, 768], BF16); gate = din("gate", [1, 1024]); x = din("x", [2048, 1024])
    w_out = din("w_out", [1024, 1024]); idb = din("idb", [128, 128], BF16)
    x1 = dout("x1", [2048, 1024])
    SCALE = 128 ** -0.5
    with ExitStack() as es:
        P = Prog(nc, es)
        ids = P.sb("ids", [128, 128], BF16); gbc = P.sb("gbc", [128, 1024], F32)
        wo = P.sb("wo", [128, 8, 1024], BF16); QTs = P.sb("QTs", [128, 6, 2048], BF16)
        KTs = P.sb("KTs", [128, NKC * 128], BF16); Vs = P.sb("Vs", [128, NKC, 129], BF16)
        yT = P.sb("yT", [128, 8, 2048], BF16); sbgs = P.sb("sbgs", [128, 16, 768], BF16)
        pts = [P.sb("pts%d" % i, [128, 512], BF16) for i in range(3)]
        rden = P.sb("rden", [128, 4], F32); On = [P.sb("On%d" % i, [128, 128], BF16) for i in range(2)]
        xt = [P.sb("xt%d" % i, [128, 1024], F32) for i in range(2)]; ot = [P.sb("ot%d" % i, [128, 1024], F32) for i in range(2)]
        st = [P.ps("st%d" % i, [128, 512], F32) for i in range(3)]
        oacc = [[P.ps("oacc%d_%d" % (s, k), [128, 512], F32) for k in range(2)] for s in range(2)]
        pmisc = P.ps("pmisc", [128, 512], F32)
        pmisc_bf = pmisc[:].bitcast(BF16)
        P.dma('sync', ids[:], idb, w=['ids'])
        P.dma('sync', gbc[:], gate.partition_broadcast(128), w=['gbc'])
        P.dma('gpsimd', wo[:], w_out.rearrange("(kc p) n -> p kc n", p=128), w=['wo'])
        P.dma('sync', QTs[:], QT.rearrange("h p t -> p h t"), w=['QTs'])
        P.dma('sync', yT[:, 0:2, :], yaT.rearrange("c p t -> p c t"), w=['yT0', 'yT1'])
        P.dma('sync', sbgs[:], sbg.rearrange("(tt p) n -> p tt n", p=128), w=['sbgs'])
        pending = [None]
        gi = 0
        for kvh in range(2):
            for q4 in range(4):
                P.dma('sync', KTs[:, q4 * 4160:(q4 + 1) * 4160], KTa[kvh, :, q4 * 4160:(q4 + 1) * 4160], w=['KT%d' % q4])
            Vflat = Vs[:].rearrange("p k d -> p (k d)")
            for q4 in range(2):
                P.dma('sync', Vflat[:, q4 * 65 * 129:(q4 + 1) * 65 * 129], Va[kvh, :, q4 * 65 * 129:(q4 + 1) * 65 * 129], w=['V%d' % q4])
            for QB in range(qbs):
                for g in range(3):
                    h = kvh * 3 + g
                    oset = gi % 2
                    gi += 1
                    def S(kc, h=h, QB=QB):
                        s = kc % 3
                        P.op('tensor', lambda e: e.matmul(st[s][:], lhsT=KTs[:, kc * 128:(kc + 1) * 128], rhs=QTs[:, h, QB * 512:(QB + 1) * 512], start=True, stop=True),
                             r=['KT%d' % (kc * 128 // 4160), 'QTs'], w=['st%d' % s])
                    def E(kc):
                        s = kc % 3
                        P.op('scalar', lambda e: e.activation(out=pts[s][:], in_=st[s][:], func=AF.Exp, scale=SCALE), r=['st%d' % s], w=['pts%d' % s])
                    def PV(kc, oset=oset):
                        s = kc % 3
                        def f(e):
                            for j in range(4):
                                ins = e.matmul(oacc[oset][j // 2][:, (j % 2) * 129:(j % 2) * 129 + 129], lhsT=pts[s][:, j * 128:(j + 1) * 128], rhs=Vs[:, kc, :],
                                               start=(kc == 0 and j % 2 == 0), stop=(kc == n_kc - 1), skip_group_check=True)
                            return ins
                        P.op('tensor', f, r=['pts%d' % s, 'V%d' % (kc // 65)], w=['oacc%d_0' % oset, 'oacc%d_1' % oset])
                    S(0)
                    if n_kc > 1: S(1)
                    for kc in range(n_kc):
                        E(kc)
                        if kc + 2 < n_kc: S(kc + 2)
                        PV(kc)
                        if kc == min(6, n_kc - 1) and pending[0] is not None:
                            pending[0](); pending[0] = None
                    def epi(h=h, QB=QB, oset=oset):
                        for j in range(4):
                            tt = QB * 4 + j
                            acc = oacc[oset][j // 2]; c0 = (j % 2) * 129
                            ores = 'oacc%d_%d' % (oset, j // 2)
                            b = j % 2
                            P.op('vector', lambda e, acc=acc, c0=c0, j=j: e.reciprocal(out=rden[:, j:j + 1], in_=acc[:, c0 + 128:c0 + 129]), r=[ores], w=['rden%d' % j])
                            P.op('vector', lambda e, acc=acc, c0=c0, j=j, tt=tt, b=b: e.scalar_tensor_tensor(out=On[b][:], in0=acc[:, c0:c0 + 128], scalar=rden[:, j:j + 1], in1=sbgs[:, tt, h * 128:(h + 1) * 128], op0=ALU.mult, op1=ALU.mult),
                                 r=[ores, 'rden%d' % j, 'sbgs'], w=['On%d' % b])
                            P.op('tensor', lambda e, b=b: e.transpose(pmisc_bf[:, b * 128:(b + 1) * 128], On[b][:], ids[:]), r=['On%d' % b, 'ids'], w=['pmisc'])
                            P.op('vector', lambda e, b=b, tt=tt: e.tensor_copy(out=yT[:, 2 + h, tt * 128:(tt + 1) * 128], in_=pmisc_bf[:, b * 128:(b + 1) * 128]), r=['pmisc'], w=['yT%d' % (2 + h)])
                    pending[0] = epi
        pending[0](); pending[0] = None
        for tt in range(qbs * 4):
            b = tt % 2
            P.dma('sync', xt[b][:], x[tt * 128:(tt + 1) * 128, :], w=['xt%d' % b])
            for half in range(2):
                def mm(e, tt=tt, half=half):
                    for ch in range(8):
                        ins = e.matmul(pmisc[:], lhsT=yT[:, ch, tt * 128:(tt + 1) * 128], rhs=wo[:, ch, half * 512:(half + 1) * 512], start=(ch == 0), stop=(ch == 7))
                    return ins
                P.op('tensor', mm, r=['yT%d' % c for c in range(8)] + ['wo'], w=['pmisc'])
                P.op('vector', lambda e, b=b, half=half: e.tensor_tensor(out=ot[b][:, half * 512:(half + 1) * 512], in0=pmisc[:], in1=gbc[:, half * 512:(half + 1) * 512], op=ALU.mult),
                     r=['pmisc', 'gbc'], w=['ot%d_%d' % (b, half)])
                P.op('gpsimd', lambda e, b=b, half=half: e.tensor_tensor(out=ot[b][:, half * 512:(half + 1) * 512], in0=ot[b][:, half * 512:(half + 1) * 512], in1=xt[b][:, half * 512:(half + 1) * 512], op=ALU.add),
                     r=['ot%d_%d' % (b, half), 'xt%d' % b], w=['ot%d_%d' % (b, half)])
            P.dma('sync', x1[tt * 128:(tt + 1) * 128, :], ot[b][:], r=['ot%d_0' % b, 'ot%d_1' % b], w=['o_x1_%d' % tt])
        P.op('sync', None, r=[k for k in P.last_w if k.startswith('o_')])
        P.emit()
    return nc


def p2_inputs(I, r1):
    x = I['x'][0]
    KT = np.concatenate([np.asarray(r1[0]['cKT'])] + [np.asarray(r1[i]['KT']) for i in range(8)], axis=2)
    Vall = np.concatenate([np.asarray(r1[0]['cV'])] + [np.asarray(r1[i]['V']) for i in range(8)], axis=0)
    Vh = Vall.reshape(NKC, 128, 2, 128).transpose(2, 1, 0, 3)
    Vp = np.ones((2, 128, NKC, 129), dtype=BF)
    Vp[..., 0:128] = Vh
    Vp = np.ascontiguousarray(Vp.reshape(2, 128, NKC * 129))
    maps = []
    for i in range(8):
        maps.append(dict(QT=np.asarray(r1[i]['QT']), KTa=np.ascontiguousarray(KT), Va=Vp, yaT=np.asarray(r1[i]['yaT']), sbg=np.asarray(r1[i]['sbg']),
                         gate=np.asarray(r1[i]['gate']), x=np.ascontiguousarray(x[i * 2048:(i + 1) * 2048]), w_out=np.ascontiguousarray(I['ev_w_out'][0]),
                         idb=np.eye(128, dtype=np.float32).astype(BF)))
    return maps


BF = ml_dtypes.bfloat16
EPS = 1e-6
L = 16384


def emit_mod(P, nc, ccT, w_mod, b_mod, gT, pn, pr, epsT):
    ccs = P.sb("ccs", [128, 16], F32); scc = P.sb("scc", [128, 16], F32); gTs = P.sb("gTs", [128, 8], F32); id2s = P.sb("id2s", [2, 2], F32)
    bms = P.sb("bms", [2, 3072], F32); msb = P.sb("msb", [2, 3072], F32); modT = P.sb("modT", [128, 24, 2], F32); gs = P.sb("gs", [128, 8, 2], F32)
    P.dma('sync', ccs[:], ccT, w=['cc']); P.dma('sync', gTs[:], gT, w=['gT'])
    P.dma('sync', bms[:], b_mod.partition_broadcast(2), w=['bm'])
    P.op('vector', lambda e: e.memset(id2s[:], 0.0), w=['id2'])
    P.op('vector', lambda e: e.memset(id2s[0:1, 0:1], 1.0), w=['id2'])
    P.op('scalar', lambda e: e.activation(out=scc[:], in_=ccs[:], func=AF.Silu), r=['cc'], w=['scc'])
    return ccs, scc, gTs, id2s, bms, msb, modT, gs


def build_p3():
    nc = bass.Bass("TRN2", target_bir_lowering=False)
    def din(name, shape, dt=F32): return nc.dram_tensor(name, list(shape), dt, kind="ExternalInput").ap()
    def dout(name, shape, dt=F32): return nc.dram_tensor(name, list(shape), dt, kind="ExternalOutput").ap()
    x = din("x", [2048, 1024]); xh = din("xh", [2, 1024])
    ccT = din("ccT", [128, 16]); w_mod = din("w_mod", [1024, 3072]); b_mod = din("b_mod", [1, 3072]); gT = din("gT", [128, 8])
    w_in = din("w_in", [1024, 3584]); convw = din("convw", [128, 18, 4]); hmask = din("hmask", [128, 2])
    idb = din("idb", [128, 128], BF16); id2 = din("id2", [2, 2]); F256 = din("F256", [128, 2, 512], BF16)
    ucT = dout("ucT", [2304, 2048], BF16); shgT = dout("shgT", [768, 2048], BF16); ZT = dout("ZT", [512, 2048], BF16); sfgT = dout("sfgT", [256, 2048], BF16)
    gate = dout("gate", [1, 1024])
    with ExitStack() as es:
        P = Prog(nc, es)
        ids = P.sb("ids", [128, 128], BF16); id2s = P.sb("id2s", [2, 2], F32); gTs = P.sb("gTs", [128, 8], F32)
        hmasks = P.sb("hmasks", [128, 2], F32); cws = P.sb("cws", [128, 18, 4], F32); F256s = P.sb("F256s", [128, 2, 512], BF16)
        epsT = P.sb("epsT", [128, 1], F32); ccs = P.sb("ccs", [128, 16], F32); scc = P.sb("scc", [128, 16], F32)
        bms = P.sb("bms", [2, 3072], F32); msb = P.sb("msb", [2, 3072], F32); modT = P.sb("modT", [128, 24, 2], F32); gs = P.sb("gs", [128, 8, 2], F32)
        for (t, src, nm) in [(ids, idb, 'ids'), (id2s, id2, 'id2'), (gTs, gT, 'gT'), (hmasks, hmask, 'hmask'), (cws, convw, 'cw'), (ccs, ccT, 'cc'), (F256s, F256, 'F256')]:
            P.dma('sync', t[:], src, w=[nm])
        P.dma('sync', bms[:], b_mod.partition_broadcast(2), w=['bm'])
        P.op('vector', lambda e: e.memset(epsT[:], EPS), w=['eps'])
        P.op('scalar', lambda e: e.activation(out=scc[:], in_=ccs[:], func=AF.Silu), r=['cc'], w=['scc'])
        wm = [P.sb("wm%d" % i, [128, 8, 128], F32) for i in range(2)]
        pn = P.ps("pn", [128, 512], F32); pr = P.ps("pr", [128, 512], F32)
        wmv = w_mod.rearrange("(kc p) n -> p kc n", p=128)
        for j in range(24):
            b = j % 2
            P.dma('sync', wm[b][:], wmv[:, :, j * 128:(j + 1) * 128], w=['wm%d' % b])
            def mm(e, j=j, b=b):
                for kc in range(8):
                    ins = e.matmul(pn[0:2, (j % 4) * 128:(j % 4 + 1) * 128], lhsT=scc[:, kc * 2:kc * 2 + 2], rhs=wm[b][:, kc, :], start=(kc == 0), stop=(kc == 7))
                return ins
            P.op('tensor', mm, r=['scc', 'wm%d' % b], w=['pn'])
            P.op('vector', lambda e, j=j: e.tensor_tensor(out=msb[:, j * 128:(j + 1) * 128], in0=pn[0:2, (j % 4) * 128:(j % 4 + 1) * 128], in1=bms[:, j * 128:(j + 1) * 128], op=ALU.add),
                 r=['pn', 'bm'], w=['msb%d' % j])
        def tr(e):
            for j in range(24):
                ins = e.matmul(pr[:, 2 * j:2 * j + 2], lhsT=msb[:, j * 128:(j + 1) * 128], rhs=id2s[:], start=True, stop=True)
            return ins
        P.op('tensor', tr, r=['msb%d' % j for j in range(24)] + ['id2'], w=['pr'])
        P.op('vector', lambda e: e.tensor_copy(out=modT[:].rearrange("p a b -> p (a b)"), in_=pr[:, 0:48]), r=['pr'], w=['modT'])
        P.op('vector', lambda e: e.scalar_tensor_tensor(out=gs[:, :, 0], in0=modT[:, 8:16, 0], scalar=1.0, in1=gTs[:], op0=ALU.add, op1=ALU.mult), r=['modT', 'gT'], w=['gs0'])
        P.dma('sync', gate, msb[0:1, 2048:3072], r=['msb%d' % j for j in range(16, 24)], w=['o_gate'])
        wi = P.sb("wi", [128, 8, 3584], BF16)
        wiv = w_in.rearrange("(kc p) n -> p kc n", p=128)
        for kc in range(8):
            P.dma('gpsimd', wi[:, kc, :], wiv[:, kc, :], w=['wi%d' % kc])
        WI = ['wi%d' % kc for kc in range(8)]
        hT = P.sb("hT", [128, 8, 2050], BF16)
        xt = [P.sb("xt%d" % i, [128, 1024], F32) for i in range(2)]
        xn = [P.sb("xn%d" % i, [128, 1024], BF16) for i in range(2)]
        junk = P.sb("junk", [128, 1024], BF16)
        ss = [P.sb("ss%d" % i, [128, 2], F32) for i in range(2)]
        ptr = [P.ps("ptr%d" % i, [128, 8, 128], BF16) for i in range(2)]
        tiles = [(x[i * 128:(i + 1) * 128, :], 128, i * 128, 'hT%d' % i) for i in range(16)]
        tiles.append((xh, 2, 2048, 'hT16'))
        for ti, (src, n, c0, res) in enumerate(tiles):
            b = ti % 2
            P.dma('sync', xt[b][0:n, :], src, w=['xt%d' % b])
            P.op('scalar', lambda e, b=b, n=n: e.activation(out=junk[0:n, :], in_=xt[b][0:n, :], func=AF.Square, accum_out=ss[b][0:n, 0:1]), r=['xt%d' % b], w=['junk', 'ssa%d' % b])
            P.op('scalar', lambda e, b=b, n=n: e.activation(out=ss[b][0:n, 1:2], in_=ss[b][0:n, 0:1], func=AF.Sqrt, scale=1.0 / 1024, bias=epsT[0:n, :]), r=['ssa%d' % b, 'eps'], w=['ssb%d' % b])
            P.op('vector', lambda e, b=b, n=n: e.reciprocal(out=ss[b][0:n, 1:2], in_=ss[b][0:n, 1:2]), r=['ssb%d' % b], w=['ssb%d' % b])
            P.op('vector', lambda e, b=b, n=n: e.tensor_scalar(out=xn[b][0:n, :], in0=xt[b][0:n, :], scalar1=ss[b][0:n, 1:2], scalar2=None, op0=ALU.mult), r=['xt%d' % b, 'ssb%d' % b], w=['xn%d' % b])
            def trs(e, b=b, n=n):
                for kc in range(8):
                    ins = e.transpose(ptr[b][:, kc, 0:n], xn[b][0:n, kc * 128:(kc + 1) * 128], ids[0:n, 0:n])
                return ins
            P.op('tensor', trs, r=['xn%d' % b, 'ids'], w=['ptr%d' % b])
            if ti % 2 == 0:
                def ev(e, b=b, n=n, c0=c0):
                    for kc in range(8):
                        ins = e.tensor_scalar(out=hT[:, kc, c0:c0 + n], in0=ptr[b][:, kc, 0:n], scalar1=gs[:, kc, 0:1], scalar2=modT[:, kc, 0:1], op0=ALU.mult, op1=ALU.add)
                    return ins
                P.op('vector', ev, r=['ptr%d' % b, 'gs0', 'modT'], w=[res])
            else:
                def ev(e, b=b, n=n, c0=c0):
                    for kc in range(8):
                        ins = e.activation(out=hT[:, kc, c0:c0 + n], in_=ptr[b][:, kc, 0:n], func=AF.Identity, scale=gs[:, kc, 0:1], bias=modT[:, kc, 0:1])
                    return ins
                P.op('scalar', ev, r=['ptr%d' % b, 'gs0', 'modT'], w=[res])
        pf = [P.ps("pf%d" % i, [128, 512], F32) for i in range(3)]
        jobn = [0]
        def proj(colchunk, c0, n, rres):
            b = jobn[0] % 3
            jobn[0] += 1
            def mm(e):
                for kc in range(8):
                    ins = e.matmul(pf[b][:, 0:n], lhsT=wi[:, kc, colchunk * 128:(colchunk + 1) * 128], rhs=hT[:, kc, c0:c0 + n], start=(kc == 0), stop=(kc == 7))
                return ins
            P.op('tensor', mm, r=rres + WI, w=['pf%d' % b])
            return b
        def hres(tb): return ['hT%d' % i for i in range(tb * 4, tb * 4 + 4)]
        ub = [P.sb("ub%d" % i, [128, 2050], F32) for i in range(2)]; acc = [P.sb("acc%d" % i, [128, 2048], F32) for i in range(2)]
        ucs = [P.sb("ucs%d" % i, [128, 2048], BF16) for i in range(2)]
        stg = [P.sb("stg%d" % i, [128, 512], BF16) for i in range(3)]
        uT = P.sb("uT", [128, 2, 2048], BF16)
        for ch in range(18):
            ubi = ch % 2
            u = ub[ubi]
            b = proj(ch, 2048, 2, ['hT16'])
            P.op('vector', lambda e, b=b, u=u: e.tensor_tensor(out=u[:, 0:1], in0=pf[b][:, 0:1], in1=hmasks[:, 0:1], op=ALU.mult), r=['pf%d' % b, 'hmask'], w=['ub%dh' % ubi])
            P.op('vector', lambda e, b=b, u=u: e.tensor_tensor(out=u[:, 2049:2050], in0=pf[b][:, 1:2], in1=hmasks[:, 1:2], op=ALU.mult), r=['pf%d' % b, 'hmask'], w=['ub%dh' % ubi])
            for tb in range(4):
                b = proj(ch, tb * 512, 512, hres(tb))
                P.op('scalar', lambda e, b=b, u=u, tb=tb: e.activation(out=u[:, 1 + tb * 512:1 + (tb + 1) * 512], in_=pf[b][:], func=AF.Copy), r=['pf%d' % b], w=['ub%d_%d' % (ubi, tb)])
            ures = ['ub%dh' % ubi] + ['ub%d_%d' % (ubi, tb) for tb in range(4)]
            a = acc[ubi]; o = ucs[ubi]
            P.op('vector', lambda e, u=u, a=a, ch=ch: e.tensor_scalar(out=a[:], in0=u[:, 1:2049], scalar1=cws[:, ch, 1:2], scalar2=cws[:, ch, 3:4], op0=ALU.mult, op1=ALU.add), r=ures + ['cw'], w=['acc%d' % ubi])
            P.op('vector', lambda e, u=u, a=a, ch=ch: e.scalar_tensor_tensor(out=a[:], in0=u[:, 0:2048], scalar=cws[:, ch, 0:1], in1=a[:], op0=ALU.mult, op1=ALU.add), r=ures + ['cw', 'acc%d' % ubi], w=['acc%d' % ubi])
            P.op('vector', lambda e, u=u, a=a, o=o, ch=ch: e.scalar_tensor_tensor(out=o[:], in0=u[:, 2:2050], scalar=cws[:, ch, 2:3], in1=a[:], op0=ALU.mult, op1=ALU.add), r=ures + ['cw', 'acc%d' % ubi], w=['ucs%d' % ubi])
            P.dma('sync', ucT[ch * 128:(ch + 1) * 128, :], o[:], r=['ucs%d' % ubi], w=['o_uc%d' % ch])
        for ch in range(18, 24):
            for tb in range(4):
                b = proj(ch, tb * 512, 512, hres(tb)); sb_ = jobn[0] % 3
                P.op('scalar', lambda e, b=b, sb_=sb_: e.activation(out=stg[sb_][:], in_=pf[b][:], func=AF.Silu), r=['pf%d' % b], w=['stg%d' % sb_])
                P.dma('sync', shgT[(ch - 18) * 128:(ch - 17) * 128, tb * 512:(tb + 1) * 512], stg[sb_][:], r=['stg%d' % sb_], w=['o_shg%d_%d' % (ch, tb)])
        for ch in range(26, 28):
            for tb in range(4):
                b = proj(ch, tb * 512, 512, hres(tb)); sb_ = jobn[0] % 3
                P.op('scalar', lambda e, b=b, sb_=sb_: e.activation(out=stg[sb_][:], in_=pf[b][:], func=AF.Silu), r=['pf%d' % b], w=['stg%d' % sb_])
                P.dma('sync', sfgT[(ch - 26) * 128:(ch - 25) * 128, tb * 512:(tb + 1) * 512], stg[sb_][:], r=['stg%d' % sb_], w=['o_sfg%d_%d' % (ch, tb)])
        for ch in range(24, 26):
            for tb in range(4):
                b = proj(ch, tb * 512, 512, hres(tb))
                P.op('vector', lambda e, b=b, ch=ch, tb=tb: e.tensor_copy(out=uT[:, ch - 24, tb * 512:(tb + 1) * 512], in_=pf[b][:]), r=['pf%d' % b], w=['uT%d_%d' % (ch - 24, tb)])
        for tb in range(4):
            for oc in range(4):
                b = jobn[0] % 3; jobn[0] += 1; sb_ = jobn[0] % 3
                def mm(e, b=b, oc=oc, tb=tb):
                    for cc in range(2):
                        ins = e.matmul(pf[b][:], lhsT=F256s[:, cc, oc * 128:(oc + 1) * 128], rhs=uT[:, cc, tb * 512:(tb + 1) * 512], start=(cc == 0), stop=(cc == 1))
                    return ins
                P.op('tensor', mm, r=['F256', 'uT0_%d' % tb, 'uT1_%d' % tb], w=['pf%d' % b])
                P.op('vector', lambda e, b=b, sb_=sb_: e.tensor_copy(out=stg[sb_][:], in_=pf[b][:]), r=['pf%d' % b], w=['stg%d' % sb_])
                P.dma('sync', ZT[oc * 128:(oc + 1) * 128, tb * 512:(tb + 1) * 512], stg[sb_][:], r=['stg%d' % sb_], w=['o_Z%d_%d' % (oc, tb)])
        P.op('sync', None, r=[k for k in P.last_w if k.startswith('o_')])
        P.emit()
    return nc


def fm(v):
    return np.ascontiguousarray(np.asarray(v, np.float32).reshape(8, 128).T)


def p3_inputs(I, x1):
    cc = np.stack([fm(I['c'][0]), fm(I['c_ctx'])], axis=-1).reshape(128, 16)
    cw = np.concatenate([I['hy_conv_w'][0], I['hy_conv_b'][0][None, :]], axis=0)
    convw = np.ascontiguousarray(cw.reshape(4, 18, 128).transpose(2, 1, 0))
    c = np.arange(256)
    ang = 2 * np.pi * np.outer(c, c) / 256.0
    nrm = 1.0 / np.sqrt(256.0 * L)
    Fm = np.concatenate([np.cos(ang) * nrm, -np.sin(ang) * nrm], axis=1)
    F256 = np.ascontiguousarray(Fm.reshape(2, 128, 512).transpose(1, 0, 2)).astype(BF)
    shared = dict(ccT=np.ascontiguousarray(cc), w_mod=np.ascontiguousarray(I['w_mod'][1]), b_mod=np.ascontiguousarray(I['b_mod'][1:2]), gT=fm(I['norm_g'][1]),
                  w_in=np.ascontiguousarray(I['od_w_in'][0]), convw=convw, idb=np.eye(128, dtype=np.float32).astype(BF), id2=np.eye(2, dtype=np.float32), F256=F256)
    maps = []
    for i in range(8):
        t0 = i * 2048
        xh = np.zeros((2, 1024), np.float32); hm = np.zeros((128, 2), np.float32)
        if i > 0:
            xh[0] = x1[t0 - 1]; hm[:, 0] = 1.0
        if i < 7:
            xh[1] = x1[t0 + 2048]; hm[:, 1] = 1.0
        maps.append(dict(shared, x=np.ascontiguousarray(x1[t0:t0 + 2048]), xh=xh, hmask=hm))
    return maps


BF = ml_dtypes.bfloat16
EPS = 1e-6
L = 16384
NG = 3
GC = 32


def build_p4(ngroups=NG, do_fourier=True, nch=GC):
    nc = bass.Bass("TRN2", target_bir_lowering=False)
    def din(name, shape, dt=F32): return nc.dram_tensor(name, list(shape), dt, kind="ExternalInput").ap()
    def dout(name, shape, dt=F32): return nc.dram_tensor(name, list(shape), dt, kind="ExternalOutput").ap()
    ucl = din("ucl", [9, 128, GC * 128], BF16)
    zl = din("zl", [2, 128, GC * 128], BF16)
    dec = din("dec", [NG, 128, GC * 128])
    embP = din("embP", [33, L]); w1 = din("w1", [33, 64]); w2 = din("w2", [64, 64]); mlpv = din("mlpv", [64, 4])
    w3g = din("w3g", [64, NG * 2 * 64]); skipb = din("skipb", [1, 2 * 96])
    F1 = din("F1", [128, 1024], BF16)
    W128 = din("W128", [128, 3, 128], BF16)
    VAB = din("VAB", [128, 2, 256], BF16)
    UU = din("UU", [128, 2, 2, 128], BF16)
    TT = din("TT", [128, 2, 512])
    TI = din("TI", [128, 2, 512])
    WF = din("WF", [128, 2, 256], BF16)
    TF = din("TF", [128, 2, 256])
    ones = din("ones", [128, 128])
    ycT = dout("ycT", [96, L], BF16); yfT = dout("yfT", [32, L], BF16)
    TWO_PI = 2.0 * np.pi
    with ExitStack() as es:
        P = Prog(nc, es)
        F1s = P.sb("F1s", [128, 1024], BF16); W128s = P.sb("W128s", [128, 3, 128], BF16); VABs = P.sb("VABs", [128, 2, 256], BF16)
        UUs = P.sb("UUs", [128, 2, 2, 128], BF16); TTs = P.sb("TTs", [128, 2, 512], F32); TIs = P.sb("TIs", [128, 2, 512], F32)
        WFs = P.sb("WFs", [128, 2, 256], BF16); TFs = P.sb("TFs", [128, 2, 256], F32); oness = P.sb("oness", [128, 128], F32)
        w1s = P.sb("w1s", [33, 64], F32); w2s = P.sb("w2s", [64, 64], F32); mlps = P.sb("mlps", [64, 4], F32); w3s = P.sb("w3s", [64, NG * 2 * 64], BF16)
        skb = P.sb("skb", [128, 2 * 96], F32); npi = P.sb("npi", [64, 1], F32)
        for (t, src, nm) in [(F1s, F1, 'F1'), (W128s, W128, 'W128'), (VABs, VAB, 'VAB'), (UUs, UU, 'UU'), (TTs, TT, 'TT'), (TIs, TI, 'TI'), (WFs, WF, 'WF'), (TFs, TF, 'TF'),
                             (oness, ones, 'ones'), (w1s, w1, 'w1'), (w2s, w2, 'w2'), (mlps, mlpv, 'mlpv')]:
            P.dma('sync', t[:], src, w=[nm])
        P.dma('gpsimd', w3s[:], w3g, w=['w3'])
        P.dma('sync', skb[:], skipb.partition_broadcast(128), w=['skb'])
        P.op('vector', lambda e: e.memset(npi[:], -np.pi), w=['npi'])
        P.op('vector', lambda e: e.tensor_scalar(out=mlps[:, 3:4], in0=mlps[:, 2:3], scalar1=1.0 / TWO_PI, scalar2=None, op0=ALU.mult), r=['mlpv'], w=['mlpv'])
        psA = [P.ps("psA%d" % i, [128, 512], F32) for i in range(2)]; psB = [P.ps("psB%d" % i, [128, 512], F32) for i in range(2)]
        psC = [P.ps("psC%d" % i, [128, 512], F32) for i in range(2)]; psD = [P.ps("psD%d" % i, [128, 512], F32) for i in range(2)]
        hdnP = P.sb("hdnP", [64, L], BF16)
        embs = [P.sb("embs%d" % i, [33, 256], F32) for i in range(2)]
        tiq = [P.sb("tiq%d" % i, [64, 256], mybir.dt.int32) for i in range(2)]; tfq = [P.sb("tfq%d" % i, [64, 256], F32) for i in range(2)]
        ta = [P.sb("ta%d" % i, [64, 256], F32) for i in range(2)]; h1 = [P.sb("h1_%d" % i, [64, 256], F32) for i in range(2)]
        for blk in range(L // 256):
            b = blk % 2
            P.dma('sync', embs[b][:], embP[:, blk * 256:(blk + 1) * 256], w=['embs%d' % b])
            P.op('tensor', lambda e, b=b: e.matmul(psA[b][0:64, 0:256], lhsT=w1s[:], rhs=embs[b][:], start=True, stop=True), r=['w1', 'embs%d' % b], w=['psA%d' % b])
            P.op('vector', lambda e, b=b: e.tensor_scalar(out=ta[b][:], in0=psA[b][0:64, 0:256], scalar1=mlps[:, 0:1], scalar2=mlps[:, 3:4], op0=ALU.add, op1=ALU.mult), r=['psA%d' % b, 'mlpv'], w=['ta%d' % b])
            P.op('vector', lambda e, b=b: e.tensor_copy(out=tiq[b][:], in_=ta[b][:]), r=['ta%d' % b], w=['tiq%d' % b])
            P.op('vector', lambda e, b=b: e.tensor_copy(out=tfq[b][:], in_=tiq[b][:]), r=['tiq%d' % b], w=['tfq%d' % b])
            P.op('vector', lambda e, b=b: e.tensor_tensor(out=ta[b][:], in0=ta[b][:], in1=tfq[b][:], op=ALU.subtract), r=['ta%d' % b, 'tfq%d' % b], w=['ta%d' % b])
            P.op('scalar', lambda e, b=b: e.activation(out=h1[b][:], in_=ta[b][:], func=AF.Sin, scale=TWO_PI), r=['ta%d' % b], w=['h1_%d' % b])
            P.op('tensor', lambda e, b=b: e.matmul(psB[b][0:64, 0:256], lhsT=w2s[:], rhs=h1[b][:], start=True, stop=True), r=['w2', 'h1_%d' % b], w=['psB%d' % b])
            P.op('vector', lambda e, b=b: e.tensor_scalar(out=ta[b][:], in0=psB[b][0:64, 0:256], scalar1=mlps[:, 1:2], scalar2=mlps[:, 3:4], op0=ALU.add, op1=ALU.mult), r=['psB%d' % b, 'mlpv'], w=['ta%d' % b])
            P.op('vector', lambda e, b=b: e.tensor_copy(out=tiq[b][:], in_=ta[b][:]), r=['ta%d' % b], w=['tiq%d' % b])
            P.op('vector', lambda e, b=b: e.tensor_copy(out=tfq[b][:], in_=tiq[b][:]), r=['tiq%d' % b], w=['tfq%d' % b])
            P.op('vector', lambda e, b=b: e.tensor_tensor(out=ta[b][:], in0=ta[b][:], in1=tfq[b][:], op=ALU.subtract), r=['ta%d' % b, 'tfq%d' % b], w=['ta%d' % b])
            P.op('scalar', lambda e, b=b, blk=blk: e.activation(out=hdnP[:, blk * 256:(blk + 1) * 256], in_=ta[b][:], func=AF.Sin, scale=TWO_PI), r=['ta%d' % b], w=['hdn%d' % blk])
        HDN = ['hdn%d' % blk for blk in range(L // 256)]
        m1 = [P.sb("m1_%d" % i, [128, 512], F32) for i in range(3)]; m2 = [P.sb("m2_%d" % i, [128, 512], F32) for i in range(3)]
        t2s = [P.sb("t2s%d" % i, [128, 512], F32) for i in range(2)]
        Bt = [P.sb("Bt%d" % i, [128, 512], BF16) for i in range(4)]; Yt = [P.sb("Yt%d" % i, [128, 512], BF16) for i in range(2)]
        Gp = [P.sb("Gp%d" % i, [128, 2, 2, 4, 128], BF16) for i in range(2)]
        cnt = {'s': 0, 'm': 0}
        def next_m():
            cnt['m'] += 1
            return cnt['m'] % 3
        def cmul(ps, psres, tabA, tabB, tabres, outr, outi, outres, conj=False):
            b = next_m()
            m1o = m1[b][:]; m2o = m2[b][:]
            if len(tabA.shape) == 3:
                ps = ps.rearrange("p (r n) -> p r n", r=2); m1o = m1o.rearrange("p (r n) -> p r n", r=2); m2o = m2o.rearrange("p (r n) -> p r n", r=2)
            P.op('vector', lambda e: e.tensor_tensor(out=m1o, in0=ps, in1=tabA, op=ALU.mult), r=[psres] + tabres, w=['m1_%d' % b])
            P.op('vector', lambda e: e.tensor_tensor(out=m2o, in0=ps, in1=tabB, op=ALU.mult), r=[psres] + tabres, w=['m2_%d' % b])
            if not conj:
                P.op('gpsimd', lambda e: e.tensor_tensor(out=outr, in0=m1[b][:, 0:256], in1=m2[b][:, 256:512], op=ALU.subtract), r=['m1_%d' % b, 'm2_%d' % b], w=[outres + 'r'])
                P.op('gpsimd', lambda e: e.tensor_tensor(out=outi, in0=m2[b][:, 0:256], in1=m1[b][:, 256:512], op=ALU.add), r=['m1_%d' % b, 'm2_%d' % b], w=[outres + 'i'])
            else:
                P.op('gpsimd', lambda e: e.tensor_tensor(out=outr, in0=m1[b][:, 0:256], in1=m2[b][:, 256:512], op=ALU.add), r=['m1_%d' % b, 'm2_%d' % b], w=[outres + 'r'])
                P.op('gpsimd', lambda e: e.tensor_tensor(out=outi, in0=m1[b][:, 256:512], in1=m2[b][:, 0:256], op=ALU.subtract), r=['m1_%d' % b, 'm2_%d' % b], w=[outres + 'i'])
        def fwd_step1(src_ap, srcres, conj=False):
            pa = cnt['s'] % 2; b = cnt['s'] % 4; cnt['s'] += 1
            f1 = F1s[:, 512:1024] if conj else F1s[:, 0:512]
            P.op('tensor', lambda e: e.matmul(psA[pa][:], lhsT=src_ap, rhs=f1, start=True, stop=True), r=list(srcres) + ['F1'], w=['psA%d' % pa])
            cmul(psA[pa][:], 'psA%d' % pa, TTs[:, 0, :], TTs[:, 1, :], ['TT'], Bt[b][:, 0:256], Bt[b][:, 256:512], 'Bt%d' % b, conj=conj)
            return b
        def fwd_step3(b, pb, first, last, conj=False):
            Wi_re = W128s[:, 1, :] if conj else W128s[:, 2, :]
            Wi_im = W128s[:, 2, :] if conj else W128s[:, 1, :]
            def f(e):
                e.matmul(psB[pb][:], lhsT=W128s[:, 0, :], rhs=Bt[b][:], start=first, stop=False, skip_group_check=True)
                e.matmul(psB[pb][:, 0:256], lhsT=Wi_re, rhs=Bt[b][:, 256:512], start=False, stop=False, skip_group_check=True)
                return e.matmul(psB[pb][:, 256:512], lhsT=Wi_im, rhs=Bt[b][:, 0:256], start=False, stop=last, skip_group_check=True)
            P.op('tensor', f, r=['Bt%dr' % b, 'Bt%di' % b, 'W128'], w=['psB%d' % pb])
        for g in range(ngroups):
            vt = P.sb("vt%d" % g, [128, GC, 128], BF16) if g == 0 else vt
            if g == 0:
                x1t = P.sb("x1t", [128, GC, 128], BF16); x2t = P.sb("x2t", [128, GC, 128], BF16); zt = P.sb("zt", [128, GC, 128], BF16); yt = P.sb("yt", [128, GC, 128], BF16)
                dect = P.sb("dect", [128, GC, 128], F32); hfb = P.sb("hfb", [128, 2, GC, 128], BF16); Kf = P.sb("Kf", [128, GC, 512], BF16)
                asum = P.sb("asum", [128, 2 * GC], F32); asum2 = P.sb("asum2", [128, GC], F32); rn = P.sb("rn", [128, GC], F32)
            fl = lambda t: t[:].rearrange("p c n -> p (c n)")
            if nch < GC:
                P.op('vector', lambda e: e.memset(yt[:], 0.0), w=['yt']); P.op('vector', lambda e: e.memset(zt[:], 0.0), w=['zt'])
            P.dma('sync', fl(vt), ucl[0 * 3 + g], w=['vt']); P.dma('sync', fl(x1t), ucl[1 * 3 + g], w=['x1t']); P.dma('sync', fl(x2t), ucl[2 * 3 + g], w=['x2t'])
            P.dma('sync', fl(dect), dec[g], w=['dect'])
            for o in range(2):
                for nb in range(16):
                    pb = nb % 2
                    def f(e, nb=nb, pb=pb, g=g, o=o):
                        for i in range(8):
                            n2 = nb * 8 + i
                            ins = e.matmul(psD[pb][:, i * 64:(i + 1) * 64], lhsT=hdnP[:, n2 * 128:(n2 + 1) * 128], rhs=w3s[:, (g * 2 + o) * 64:(g * 2 + o + 1) * 64], start=True, stop=True)
                        return ins
                    P.op('tensor', f, r=HDN + ['w3'], w=['psD%d' % pb])
                    for d_ in range(2):
                        src = psD[pb][:].rearrange("p (n d c) -> p d c n", n=8, d=2, c=GC)[:, d_]
                        P.op('vector', lambda e, src=src, d_=d_, nb=nb: e.tensor_tensor(out=hfb[:, d_, :, nb * 8:(nb + 1) * 8], in0=src, in1=dect[:, :, nb * 8:(nb + 1) * 8], op=ALU.mult),
                             r=['psD%d' % pb, 'dect'], w=['hfb'])
                P.op('vector', lambda e: e.tensor_reduce(out=asum[:], in_=hfb[:].rearrange("p d c n -> p (d c) n"), axis=AX.X, op=ALU.add, apply_absolute_value=True), r=['hfb'], w=['asum'])
                P.op('vector', lambda e: e.tensor_tensor(out=asum2[:], in0=asum[:, 0:GC], in1=asum[:, GC:2 * GC], op=ALU.add), r=['asum'], w=['asum2'])
                P.op('tensor', lambda e: e.matmul(psD[0][:, 0:GC], lhsT=oness[:], rhs=asum2[:], start=True, stop=True), r=['asum2', 'ones'], w=['psD0'])
                P.op('vector', lambda e: e.tensor_scalar(out=rn[:], in0=psD[0][:, 0:GC], scalar1=EPS, scalar2=None, op0=ALU.add), r=['psD0'], w=['rn'])
                P.op('vector', lambda e: e.reciprocal(out=rn[:], in_=rn[:]), r=['rn'], w=['rn'])
                P.op('vector', lambda e: e.memset(hfb[0:1, 1, :, 0:1], 0.0), r=['asum'], w=['hfb'])
                prev = None
                for c in range(nch + 1):
                    cur = None
                    if c < nch:
                        bf_ = fwd_step1(hfb[:, 0, c, :], ['hfb'])
                        bb_ = fwd_step1(hfb[:, 1, c, :], ['hfb'], conj=True)
                        cur = (c, bf_, bb_)
                    if prev is not None:
                        pc, pbf, pbb = prev
                        pb = pc % 2
                        fwd_step3(pbf, pb, True, False)
                        fwd_step3(pbb, pb, False, True, conj=True)
                        P.op('scalar', lambda e, pc=pc, pb=pb: e.activation(out=Kf[:, pc, :], in_=psB[pb][:], func=AF.Copy, scale=rn[:, pc:pc + 1]), r=['psB%d' % pb, 'rn'], w=['Kf%d' % pc])
                    prev = cur
                src_t = vt if o == 0 else zt
                gate_t = x1t if o == 0 else x2t
                dst_t = zt if o == 0 else yt
                sres = 'vt' if o == 0 else 'zt'
                gres = 'x1t' if o == 0 else 'x2t'
                dres = 'zt' if o == 0 else 'yt'
                st1 = {}; st2 = {}
                for s in range(nch + 3):
                    if s < nch:
                        c = s
                        st1[c] = fwd_step1(src_t[:, c, :], [sres])
                    if 0 <= s - 1 < nch:
                        c = s - 1; b = st1[c]; pb = c % 2
                        fwd_step3(b, pb, True, True)
                        yb = c % 2
                        cmul(psB[pb][:], 'psB%d' % pb, Kf[:, c, 0:256].unsqueeze(1).to_broadcast([128, 2, 256]), Kf[:, c, 256:512].unsqueeze(1).to_broadcast([128, 2, 256]), ['Kf%d' % c],
                             Yt[yb][:, 0:256], Yt[yb][:, 256:512], 'Yt%d' % yb)
                        st2[c] = yb
                    if 0 <= s - 2 < nch:
                        c = s - 2; yb = st2[c]; pc_ = c % 2; gq = (c // 4) % 2; ci = c % 4
                        def f(e, yb=yb, pc_=pc_):
                            for q in range(2):
                                e.matmul(psC[pc_][:, q * 256:(q + 1) * 256], lhsT=Yt[yb][:, q * 128:(q + 1) * 128], rhs=VABs[:, 0, :], start=True, stop=False)
                                ins = e.matmul(psC[pc_][:, q * 256:(q + 1) * 256], lhsT=Yt[yb][:, 256 + q * 128:256 + (q + 1) * 128], rhs=VABs[:, 1, :], start=False, stop=True)
                            return ins
                        P.op('tensor', f, r=['Yt%dr' % yb, 'Yt%di' % yb, 'VAB'], w=['psC%d' % pc_])
                        mb = next_m()
                        P.op('vector', lambda e, pc_=pc_, mb=mb: e.tensor_tensor(out=m1[mb][:], in0=psC[pc_][:], in1=TIs[:, 0, :], op=ALU.mult), r=['psC%d' % pc_, 'TI'], w=['m1_%d' % mb])
                        P.op('vector', lambda e, pc_=pc_, mb=mb: e.tensor_tensor(out=m2[mb][:], in0=psC[pc_][:], in1=TIs[:, 1, :], op=ALU.mult), r=['psC%d' % pc_, 'TI'], w=['m2_%d' % mb])
                        m1v = m1[mb][:].rearrange("p (q r n) -> p q r n", q=2, r=2); m2v = m2[mb][:].rearrange("p (q r n) -> p q r n", q=2, r=2)
                        P.op('gpsimd', lambda e, m1v=m1v, m2v=m2v, gq=gq, ci=ci: e.tensor_tensor(out=Gp[gq][:, :, 0, ci, :], in0=m1v[:, :, 0, :], in1=m2v[:, :, 1, :], op=ALU.subtract), r=['m1_%d' % mb, 'm2_%d' % mb], w=['Gp%d_%dr' % (gq, ci)])
                        P.op('gpsimd', lambda e, m1v=m1v, m2v=m2v, gq=gq, ci=ci: e.tensor_tensor(out=Gp[gq][:, :, 1, ci, :], in0=m2v[:, :, 0, :], in1=m1v[:, :, 1, :], op=ALU.add), r=['m1_%d' % mb, 'm2_%d' % mb], w=['Gp%d_%di' % (gq, ci)])
                        if ci == 3 or c == nch - 1:
                            c0 = c - ci; nci = ci + 1
                            pd = gq
                            def f(e, gq=gq, pd=pd, nci=nci):
                                k = 0
                                for q in range(2):
                                    for part in range(2):
                                        ins = e.matmul(psD[pd][:, 0:nci * 128], lhsT=UUs[:, q, part, :], rhs=Gp[gq][:, q, part, 0:nci, :].rearrange("p c n -> p (c n)"), start=(k == 0), stop=(k == 3))
                                        k += 1
                                return ins
                            P.op('tensor', f, r=['Gp%d_%d%s' % (gq, i, ri) for i in range(nci) for ri in 'ri'] + ['UU'], w=['psD%d' % pd])
                            tb_ = gq
                            for i in range(nci):
                                cc = c0 + i
                                P.op('vector', lambda e, cc=cc, i=i, pd=pd, tb_=tb_, src_t=src_t, o=o, g=g: e.scalar_tensor_tensor(out=t2s[tb_][:, i * 128:(i + 1) * 128], in0=src_t[:, cc, :], scalar=skb[:, o * 96 + g * GC + cc:o * 96 + g * GC + cc + 1], in1=psD[pd][:, i * 128:(i + 1) * 128], op0=ALU.mult, op1=ALU.add),
                                     r=[sres, 'skb', 'psD%d' % pd], w=['t2s%d' % tb_])
                            P.op('vector', lambda e, c0=c0, nci=nci, tb_=tb_, dst_t=dst_t, gate_t=gate_t: e.tensor_tensor(out=dst_t[:, c0:c0 + nci, :].rearrange("p c n -> p (c n)"), in0=t2s[tb_][:, 0:nci * 128], in1=gate_t[:, c0:c0 + nci, :].rearrange("p c n -> p (c n)"), op=ALU.mult),
                                 r=['t2s%d' % tb_, gres], w=[dres])
            P.dma('sync', ycT[g * GC:(g + 1) * GC, :].rearrange("c (n1 n2) -> n1 c n2", n2=128), yt[:], r=['yt'], w=['o_yc%d' % g])
        if do_fourier:
            zr = vt; zi = x1t; yf = zt
            P.dma('sync', zr[:].rearrange("p c n -> p (c n)"), zl[0], w=['vt']); P.dma('sync', zi[:].rearrange("p c n -> p (c n)"), zl[1], w=['x1t'])
            Bf = [P.sb("Bf%d" % i, [128, 2, 4, 128], BF16) for i in range(2)]
            for c in range(GC):
                pa = c % 2; gq = (c // 4) % 2; ci = c % 4
                def f(e, c=c, pa=pa):
                    e.matmul(psA[pa][:, 0:256], lhsT=zr[:, c, :], rhs=WFs[:, 0, :], start=True, stop=False)
                    return e.matmul(psA[pa][:, 0:256], lhsT=zi[:, c, :], rhs=WFs[:, 1, :], start=False, stop=True)
                P.op('tensor', f, r=['vt', 'x1t', 'WF'], w=['psA%d' % pa])
                mb = next_m()
                P.op('vector', lambda e, pa=pa, mb=mb: e.tensor_tensor(out=m1[mb][:, 0:256], in0=psA[pa][:, 0:256], in1=TFs[:, 0, :], op=ALU.mult), r=['psA%d' % pa, 'TF'], w=['m1_%d' % mb])
                P.op('vector', lambda e, pa=pa, mb=mb: e.tensor_tensor(out=m2[mb][:, 0:256], in0=psA[pa][:, 0:256], in1=TFs[:, 1, :], op=ALU.mult), r=['psA%d' % pa, 'TF'], w=['m2_%d' % mb])
                P.op('gpsimd', lambda e, mb=mb, gq=gq, ci=ci: e.tensor_tensor(out=Bf[gq][:, 0, ci, :], in0=m1[mb][:, 0:128], in1=m2[mb][:, 128:256], op=ALU.subtract), r=['m1_%d' % mb, 'm2_%d' % mb], w=['Bf%d_%dr' % (gq, ci)])
                P.op('gpsimd', lambda e, mb=mb, gq=gq, ci=ci: e.tensor_tensor(out=Bf[gq][:, 1, ci, :], in0=m2[mb][:, 0:128], in1=m1[mb][:, 128:256], op=ALU.add), r=['m1_%d' % mb, 'm2_%d' % mb], w=['Bf%d_%di' % (gq, ci)])
                if ci == 3:
                    c0 = c - 3; pd = gq
                    def f(e, gq=gq, pd=pd):
                        e.matmul(psD[pd][:], lhsT=W128s[:, 0, :], rhs=Bf[gq][:, 0, :, :].rearrange("p c n -> p (c n)"), start=True, stop=False)
                        return e.matmul(psD[pd][:], lhsT=W128s[:, 2, :], rhs=Bf[gq][:, 1, :, :].rearrange("p c n -> p (c n)"), start=False, stop=True)
                    P.op('tensor', f, r=['Bf%d_%d%s' % (gq, i, ri) for i in range(4) for ri in 'ri'] + ['W128'], w=['psD%d' % pd])
                    P.op('scalar', lambda e, c0=c0, pd=pd: e.activation(out=yf[:, c0:c0 + 4, :].rearrange("p c n -> p (c n)"), in_=psD[pd][:], func=AF.Copy), r=['psD%d' % pd], w=['zt'])
            P.dma('sync', yfT.rearrange("c (k2 k1) -> k2 c k1", k1=128), yf[:], r=['zt'], w=['o_yf'])
        P.op('sync', None, r=[k for k in P.last_w if k.startswith('o_')])
        P.emit()
    return nc


def layT(a):
    C = a.shape[0]
    return np.ascontiguousarray(a.reshape(C, 128, 128).transpose(1, 0, 2).reshape(128, C * 128))


def p4_consts():
    N = 2 * L
    n1 = np.arange(128)[:, None]; k1 = np.arange(256)[None, :]
    a = 2 * np.pi * n1 * k1 / 256.0
    F1r = np.cos(a); F1i = -np.sin(a)
    F1 = np.concatenate([F1r, F1i, F1r, -F1i], axis=1).astype(BF)
    n2 = np.arange(128)[:, None]; k2 = np.arange(128)[None, :]
    a = 2 * np.pi * n2 * k2 / 128.0
    Wr = np.cos(a); Wi = -np.sin(a)
    W128 = np.ascontiguousarray(np.stack([Wr, Wi, -Wi], axis=1)).astype(BF)
    VAB = np.ascontiguousarray(np.stack([np.concatenate([Wr, -Wi], axis=1), np.concatenate([Wi, Wr], axis=1)], axis=1)).astype(BF)
    k1f = np.arange(256)[:, None]; n1f = np.arange(128)[None, :]
    a = 2 * np.pi * k1f * n1f / 256.0
    Ur = np.cos(a) / N; Ui = np.sin(a) / N
    UU = np.zeros((128, 2, 2, 128))
    for q in range(2):
        UU[:, q, 0, :] = Ur[q * 128:(q + 1) * 128]; UU[:, q, 1, :] = -Ui[q * 128:(q + 1) * 128]
    UU = UU.astype(BF)
    a = 2 * np.pi * np.arange(128)[:, None] * np.arange(256)[None, :] / N
    Tr = np.cos(a); Ti = -np.sin(a)
    TT = np.ascontiguousarray(np.stack([np.concatenate([Tr, Tr], 1), np.concatenate([Ti, Ti], 1)], axis=1)).astype(np.float32)
    TI = np.zeros((128, 2, 2, 2, 128))
    for q in range(2):
        kk = (q * 128 + np.arange(128))[:, None]; nn = np.arange(128)[None, :]
        a = 2 * np.pi * kk * nn / N
        TI[:, 0, q, 0, :] = np.cos(a); TI[:, 0, q, 1, :] = np.cos(a)
        TI[:, 1, q, 0, :] = np.sin(a); TI[:, 1, q, 1, :] = np.sin(a)
    TI = TI.reshape(128, 2, 512).astype(np.float32)
    WF = np.ascontiguousarray(np.stack([np.concatenate([Wr, Wi], 1), np.concatenate([-Wi, Wr], 1)], axis=1)).astype(BF)
    a = 2 * np.pi * np.arange(128)[:, None] * np.arange(128)[None, :] / L
    Tr = np.cos(a); Ti = -np.sin(a)
    TF = np.ascontiguousarray(np.stack([np.concatenate([Tr, Tr], 1), np.concatenate([Ti, Ti], 1)], axis=1)).astype(np.float32)
    t = np.linspace(0.0, 1.0, L, dtype=np.float32)[:, None]
    w = (2.0 * np.pi * np.arange(L, dtype=np.float32)[:, None] / L).astype(np.float32)
    f = np.linspace(1e-4, 15, 16, dtype=np.float32)[None, :]
    emb = np.concatenate([t, np.cos(f * w), -np.sin(f * w)], axis=-1).astype(np.float32)
    embP = np.ascontiguousarray(emb.reshape(128, 128, 33).transpose(2, 1, 0).reshape(33, L))
    max_decay = np.log(1e-2) / 0.3; min_decay = np.log(1e-2) / 1.5
    deltas = np.linspace(min_decay, max_decay, 768, dtype=np.float32)
    decay = np.exp(-t * np.abs(deltas)[None, :]).astype(np.float32)
    return dict(F1=F1, W128=W128, VAB=VAB, UU=UU, TT=TT, TI=TI, WF=WF, TF=TF, embP=embP, ones=np.ones((128, 128), np.float32)), decay


def p4_inputs(I, ucT, ZT):
    consts, decay = p4_consts()
    fr = I['hy_freq'][0]
    w3 = I['hy_w3'][0].reshape(64, 2, 2, 768)
    maps = []
    for j in range(8):
        ucl = np.zeros((9, 128, GC * 128), dtype=BF)
        for a in range(3):
            for g in range(NG):
                ch0 = a * 768 + 96 * j + g * GC
                ucl[a * 3 + g] = layT(ucT[ch0:ch0 + GC])
        zl = np.stack([layT(ZT[32 * j:32 * j + 32]), layT(ZT[256 + 32 * j:256 + 32 * j + 32])])
        dec = np.stack([layT(np.ascontiguousarray(decay[:, 96 * j + g * GC:96 * j + (g + 1) * GC].T)) for g in range(NG)])
        w3g = np.ascontiguousarray(np.stack([w3[:, :, :, 96 * j + g * GC:96 * j + (g + 1) * GC] for g in range(NG)], axis=1).reshape(64, NG * 2 * 64))
        skipb = np.ascontiguousarray(I['hy_skip'][0][:, 96 * j:96 * j + 96].reshape(1, 192))
        mlpv = np.ascontiguousarray(np.stack([I['hy_b1'][0], I['hy_b2'][0], fr, fr], axis=1))
        maps.append(dict(consts, ucl=ucl, zl=zl, dec=dec, w1=np.ascontiguousarray(I['hy_w1'][0]), w2=np.ascontiguousarray(I['hy_w2'][0]), mlpv=mlpv, w3g=w3g, skipb=skipb))
    return maps


BF = ml_dtypes.bfloat16
EPS = 1e-6


def build_p5():
    nc = bass.Bass("TRN2", target_bir_lowering=False)
    def din(name, shape, dt=F32): return nc.dram_tensor(name, list(shape), dt, kind="ExternalInput").ap()
    def dout(name, shape, dt=F32): return nc.dram_tensor(name, list(shape), dt, kind="ExternalOutput").ap()
    ycT = din("ycT", [768, 2048], BF16); yfT = din("yfT", [256, 2048], BF16); shgT = din("shgT", [768, 2048], BF16); sfgT = din("sfgT", [256, 2048], BF16)
    x1 = din("x1", [2048, 1024]); gate = din("gate", [1, 1024]); fn_w = din("fn_w", [256, 256]); w_out = din("w_out", [1024, 1024]); fg = din("fg", [1, 1024])
    out = dout("out", [2048, 1024])
    with ExitStack() as es:
        P = Prog(nc, es)
        gbc = P.sb("gbc", [128, 1024], F32); fgbc = P.sb("fgbc", [128, 1024], F32); epsT = P.sb("epsT", [128, 1], F32)
        wo = P.sb("wo", [128, 8, 1024], BF16); fnws = P.sb("fnws", [128, 2, 256], BF16)
        yT = P.sb("yT", [128, 8, 2048], BF16); ycs = P.sb("ycs", [128, 6, 2048], BF16); gts = P.sb("gts", [128, 8, 2048], BF16); yfs = P.sb("yfs", [128, 2, 2048], BF16)
        xt = [P.sb("xt%d" % i, [128, 1024], F32) for i in range(2)]; ot = [P.sb("ot%d" % i, [128, 1024], F32) for i in range(2)]
        fo = [P.sb("fo%d" % i, [128, 1024], F32) for i in range(2)]; junk = P.sb("junk", [128, 1024], BF16); ss = [P.sb("ss%d" % i, [128, 2], F32) for i in range(2)]
        ps = [P.ps("ps%d" % i, [128, 512], F32) for i in range(4)]
        P.dma('sync', gbc[:], gate.partition_broadcast(128), w=['gbc']); P.dma('sync', fgbc[:], fg.partition_broadcast(128), w=['fgbc'])
        P.op('vector', lambda e: e.memset(epsT[:], EPS), w=['eps'])
        P.dma('gpsimd', wo[:], w_out.rearrange("(kc p) n -> p kc n", p=128), w=['wo'])
        P.dma('gpsimd', fnws[:], fn_w.rearrange("(kc p) n -> p kc n", p=128), w=['fnw'])
        P.dma('sync', ycs[:], ycT.rearrange("(c p) t -> p c t", p=128), w=['ycs'])
        P.dma('sync', gts[:, 0:6, :], shgT.rearrange("(c p) t -> p c t", p=128), w=['gts_a'])
        P.dma('sync', gts[:, 6:8, :], sfgT.rearrange("(c p) t -> p c t", p=128), w=['gts_b'])
        P.dma('sync', yfs[:], yfT.rearrange("(c p) t -> p c t", p=128), w=['yfs'])
        for c in range(6):
            eng = 'vector' if c % 2 == 0 else 'gpsimd'
            P.op(eng, lambda e, c=c: e.tensor_tensor(out=yT[:, c, :], in0=ycs[:, c, :], in1=gts[:, c, :], op=ALU.mult), r=['ycs', 'gts_a'], w=['yT%d' % c])
        k = 0
        for dch in range(2):
            for tb in range(4):
                b = k % 4; k += 1
                def mm(e, b=b, dch=dch, tb=tb):
                    for cc in range(2):
                        ins = e.matmul(ps[b][:], lhsT=fnws[:, cc, dch * 128:(dch + 1) * 128], rhs=yfs[:, cc, tb * 512:(tb + 1) * 512], start=(cc == 0), stop=(cc == 1))
                    return ins
                P.op('tensor', mm, r=['fnw', 'yfs'], w=['ps%d' % b])
                P.op('vector', lambda e, b=b, dch=dch, tb=tb: e.tensor_tensor(out=yT[:, 6 + dch, tb * 512:(tb + 1) * 512], in0=ps[b][:], in1=gts[:, 6 + dch, tb * 512:(tb + 1) * 512], op=ALU.mult),
                     r=['ps%d' % b, 'gts_b'], w=['yT%d_%d' % (6 + dch, tb)])
        YT = ['yT%d' % c for c in range(6)] + ['yT%d_%d' % (c, tb) for c in (6, 7) for tb in range(4)]
        for tt in range(16):
            b = tt % 2
            P.dma('sync', xt[b][:], x1[tt * 128:(tt + 1) * 128, :], w=['xt%d' % b])
            for half in range(2):
                pb = (tt * 2 + half) % 4
                def mm(e, tt=tt, half=half, pb=pb):
                    for ch in range(8):
                        ins = e.matmul(ps[pb][:], lhsT=yT[:, ch, tt * 128:(tt + 1) * 128], rhs=wo[:, ch, half * 512:(half + 1) * 512], start=(ch == 0), stop=(ch == 7))
                    return ins
                P.op('tensor', mm, r=YT + ['wo'], w=['ps%d' % pb])
                P.op('vector', lambda e, b=b, half=half, pb=pb: e.tensor_tensor(out=ot[b][:, half * 512:(half + 1) * 512], in0=ps[pb][:], in1=gbc[:, half * 512:(half + 1) * 512], op=ALU.mult),
                     r=['ps%d' % pb, 'gbc'], w=['ot%d_%d' % (b, half)])
                P.op('gpsimd', lambda e, b=b, half=half: e.tensor_tensor(out=ot[b][:, half * 512:(half + 1) * 512], in0=ot[b][:, half * 512:(half + 1) * 512], in1=xt[b][:, half * 512:(half + 1) * 512], op=ALU.add),
                     r=['ot%d_%d' % (b, half), 'xt%d' % b], w=['ot%d_%d' % (b, half)])
            P.op('scalar', lambda e, b=b: e.activation(out=junk[:], in_=ot[b][:], func=AF.Square, accum_out=ss[b][:, 0:1]), r=['ot%d_0' % b, 'ot%d_1' % b], w=['junk', 'ssa%d' % b])
            P.op('scalar', lambda e, b=b: e.activation(out=ss[b][:, 1:2], in_=ss[b][:, 0:1], func=AF.Sqrt, scale=1.0 / 1024, bias=epsT[:]), r=['ssa%d' % b, 'eps'], w=['ssb%d' % b])
            P.op('vector', lambda e, b=b: e.reciprocal(out=ss[b][:, 1:2], in_=ss[b][:, 1:2]), r=['ssb%d' % b], w=['ssb%d' % b])
            P.op('vector', lambda e, b=b: e.scalar_tensor_tensor(out=fo[b][:], in0=ot[b][:], scalar=ss[b][:, 1:2], in1=fgbc[:], op0=ALU.mult, op1=ALU.mult), r=['ot%d_0' % b, 'ot%d_1' % b, 'ssb%d' % b, 'fgbc'], w=['fo%d' % b])
            P.dma('sync', out[tt * 128:(tt + 1) * 128, :], fo[b][:], r=['fo%d' % b], w=['o_out%d' % tt])
        P.op('sync', None, r=[k_ for k_ in P.last_w if k_.startswith('o_')])
        P.emit()
    return nc


def p5_inputs(I, x1, ycT, yfT, shgT, sfgT, gate):
    maps = []
    for i in range(8):
        sl = slice(i * 2048, (i + 1) * 2048)
        maps.append(dict(ycT=np.ascontiguousarray(ycT[:, sl]), yfT=np.ascontiguousarray(yfT[:, sl]), shgT=np.ascontiguousarray(shgT[:, sl]), sfgT=np.ascontiguousarray(sfgT[:, sl]),
                         x1=np.ascontiguousarray(x1[sl]), gate=gate, fn_w=np.ascontiguousarray(I['fn_w'][0]), w_out=np.ascontiguousarray(I['od_w_out'][0]),
                         fg=np.ascontiguousarray(I['final_g'][None, :])))
    return maps


def _cat(res, key, axis):
    return np.concatenate([np.asarray(res[i][key]) for i in range(8)], axis=axis)


def kernel(**inputs):
    I = {k: np.asarray(v) for k, v in inputs.items()}
    cores = list(range(8))
    r1 = run_bass_kernel_spmd(build_p1(), p1_inputs(I), core_ids=cores).results
    r2 = run_bass_kernel_spmd(build_p2(), p2_inputs(I, r1), core_ids=cores).results
    x1 = _cat(r2, 'x1', 0)
    r3 = run_bass_kernel_spmd(build_p3(), p3_inputs(I, x1), core_ids=cores).results
    ucT = _cat(r3, 'ucT', 1); ZT = _cat(r3, 'ZT', 1); shgT = _cat(r3, 'shgT', 1); sfgT = _cat(r3, 'sfgT', 1)
    r4 = run_bass_kernel_spmd(build_p4(), p4_inputs(I, ucT, ZT), core_ids=cores).results
    ycT = _cat(r4, 'ycT', 0); yfT = _cat(r4, 'yfT', 0)
    r5 = run_bass_kernel_spmd(build_p5(), p5_inputs(I, x1, ycT, yfT, shgT, sfgT, np.asarray(r3[0]['gate'])), core_ids=cores).results
    out = _cat(r5, 'out', 0)
    return np.ascontiguousarray(out.reshape(1, 16384, 1024).astype(np.float32))
```
